# Optimizing a Trainium2 kernel written in Bass

```python
import math
import jax
import jax.numpy as jnp
from jax import lax
import numpy as np

D_MODEL = 4096
BATCH = 2
SEQ = 4096
DEPTH = 4

GRID_W = 64
CTX_LEN = 256
N_MIXERS = 4
ADA_RANK = 256
FFN_HIDDEN = math.ceil(8 * D_MODEL / 3 / 256) * 256
ALPHA = (2 * DEPTH) ** 0.25
BETA = (8 * DEPTH) ** -0.25
LN_EPS = 1e-5
SSM_GROUP = 16
SSM_GROUPS = D_MODEL // SSM_GROUP
SSM_STATE = 64
SSM_CHUNK = 128
DT_MIN = 1e-3
DT_MAX = 1e-1
DA_DIM = 128
DA_HEADS = D_MODEL // (2 * DA_DIM)
Q_BLOCK = 128
ROPE_THETA = 10000.0
CM_CHUNK = 128
CM_GROUPS = 16
CM_GW = D_MODEL // CM_GROUPS
NA_DIM = 128
NA_HEADS = D_MODEL // NA_DIM
WIN_H = 8
WIN_W = 16

kernel_name = "hybrid_s5_diffattn_chunkmlp_natten_dit"


def _layernorm(x, g, b):
    xf = x.astype(jnp.float32)
    mu = jnp.mean(xf, axis=-1, keepdims=True)
    var = jnp.mean(jnp.square(xf - mu), axis=-1, keepdims=True)
    return ((xf - mu) * lax.rsqrt(var + LN_EPS)).astype(x.dtype) * g + b


def _rmsnorm(x, g):
    xf = x.astype(jnp.float32)
    y = xf * lax.rsqrt(jnp.mean(jnp.square(xf), axis=-1, keepdims=True) + LN_EPS)
    return y.astype(x.dtype) * g


def _adaln(cond, w_down, w_up, b):
    return ((jax.nn.silu(cond) @ w_down) @ w_up + b)[:, None, :]


def _swiglu(t, w_in, w_out):
    g, u = jnp.split(t @ w_in, 2, axis=-1)
    return (jax.nn.silu(g) * u) @ w_out


def _rope(x, pos):
    n = x.shape[-1]
    half = n // 2
    inv = ROPE_THETA ** (-jnp.arange(half, dtype=jnp.float32) * 2.0 / n)
    ang = pos.astype(jnp.float32)[:, None] * inv
    ang = ang.reshape((pos.shape[0],) + (1,) * (x.ndim - 3) + (half,))
    cos = jnp.cos(ang).astype(x.dtype)
    sin = jnp.sin(ang).astype(x.dtype)
    x1, x2 = x[..., :half], x[..., half:]
    return jnp.concatenate([x1 * cos - x2 * sin, x1 * sin + x2 * cos], axis=-1)


def _rope_2d(x, row, col):
    half = x.shape[-1] // 2
    return jnp.concatenate([_rope(x[..., :half], row), _rope(x[..., half:], col)], axis=-1)


def _lin_combine(e1, e2):
    a1, b1 = e1
    a2, b2 = e2
    return a1 * a2, a2 * b1 + b2


def _s5_scan(u, a_bar, b_bar, cm, h0, reverse):
    Bsz, L, G, GC = u.shape
    T = SSM_CHUNK
    n = L // T
    ub = u.reshape(Bsz, n, T, G, GC).transpose(1, 2, 0, 3, 4)
    a_seq = jnp.broadcast_to(a_bar, (T, 1) + a_bar.shape)
    first = T - 1 if reverse else 0

    def step(h, u_blk):
        bu = jnp.einsum('tbgc,gpc->tbgp', u_blk, b_bar)
        bu = bu.at[first].add(a_bar * h)
        _, hs = lax.associative_scan(_lin_combine, (a_seq, bu), axis=0, reverse=reverse)
        y = jnp.einsum('tbgp,gcp->tbgc', hs, cm).real
        return hs[T - 1 - first], y

    h_last, ys = lax.scan(step, h0, ub, reverse=reverse)
    return ys.transpose(2, 0, 1, 3, 4).reshape(Bsz, L, G, GC), h_last


def _s5_mixer(h, hc, a_re, a_im, log_dt, b_re, b_im, c_re, c_im, d_skip, w_glu, ctx_out):
    Bsz, L, D = h.shape
    f32 = jnp.float32
    lam = lax.complex(a_re.astype(f32), a_im.astype(f32))
    dt = jnp.exp(log_dt.astype(f32))[..., None]
    a_bar = jnp.exp(lam * dt)
    b = lax.complex(b_re.astype(f32), b_im.astype(f32))
    b_bar = ((a_bar - 1.0) / lam)[..., None] * b
    cm = lax.complex(c_re.astype(f32), c_im.astype(f32))
    u = h.astype(f32).reshape(Bsz, L, SSM_GROUPS, SSM_GROUP)
    uc = hc.astype(f32).reshape(Bsz, hc.shape[1], SSM_GROUPS, SSM_GROUP)
    h0 = jnp.zeros((Bsz, SSM_GROUPS, SSM_STATE), jnp.complex64)
    yc_f, sc_f = _s5_scan(uc, a_bar[0], b_bar[0], cm, h0, False)
    yc_b, sc_b = _s5_scan(uc, a_bar[1], b_bar[1], cm, h0, True)
    y_f, _ = _s5_scan(u, a_bar[0], b_bar[0], cm, sc_f, False)
    y_b, _ = _s5_scan(u, a_bar[1], b_bar[1], cm, sc_b, True)

    def finish(y, t):
        y = (y.reshape(Bsz, t.shape[1], D) + d_skip.astype(f32) * t.astype(f32)).astype(t.dtype)
        a, g = jnp.split(jax.nn.gelu(y, approximate=False) @ w_glu, 2, axis=-1)
        return a * jax.nn.sigmoid(g)

    y_lat = finish(y_f + y_b, h)
    y_ctx = finish(yc_f + yc_b, hc) if ctx_out else None
    return y_lat, y_ctx


def _diff_attention(h, hc, row, col, w_qkv, w_o, lam_p, subln_g, lam_init, ctx_out):
    Bsz, L, D = h.shape

    def project(t):
        q, k, v = jnp.split(t @ w_qkv, 3, axis=-1)
        n = t.shape[1]
        return (q.reshape(Bsz, n, DA_HEADS, 2, DA_DIM), k.reshape(Bsz, n, DA_HEADS, 2, DA_DIM),
                v.reshape(Bsz, n, DA_HEADS, 2 * DA_DIM))

    q, k, v = project(h)
    q, k = _rope_2d(q, row, col), _rope_2d(k, row, col)
    qc, kc, vc = project(hc)
    lp = lam_p.astype(jnp.float32)
    lam = jnp.exp(jnp.sum(lp[0] * lp[1])) - jnp.exp(jnp.sum(lp[2] * lp[3])) + lam_init
    keys = jnp.concatenate([kc, k], axis=1)
    vals = jnp.concatenate([vc, v], axis=1)
    scale = DA_DIM ** -0.5

    def attend(qb, kk, vv):
        s = jnp.einsum('bqhid,bkhid->bhiqk', qb, kk).astype(jnp.float32) * scale
        p = jax.nn.softmax(s, axis=-1)
        w = (p[:, :, 0] - lam * p[:, :, 1]).astype(vv.dtype)
        return jnp.einsum('bhqk,bkhe->bqhe', w, vv)

    nblk = L // Q_BLOCK
    qb = q.reshape(Bsz, nblk, Q_BLOCK, DA_HEADS, 2, DA_DIM).swapaxes(0, 1)
    o = lax.map(lambda blk: attend(blk, keys, vals), qb)
    o = o.swapaxes(0, 1).reshape(Bsz, L, DA_HEADS, 2 * DA_DIM)

    def out(o):
        return (_rmsnorm(o, subln_g) * (1.0 - lam_init)).reshape(Bsz, o.shape[1], D) @ w_o

    y_lat = out(o)
    y_ctx = out(attend(qc, kc, vc)) if ctx_out else None
    return y_lat, y_ctx


def _chunk_mlp_seq(t, w_in, ln_g, ln_b, w_s, b_s, w_out):
    Bsz, n, D = t.shape
    u, v = jnp.split(jax.nn.gelu(t @ w_in, approximate=False), 2, axis=-1)
    v = _layernorm(v, ln_g, ln_b)
    vg = v.reshape(Bsz, n // CM_CHUNK, CM_CHUNK, CM_GROUPS, CM_GW)
    vm = jnp.einsum('gst,bntgc->bnsgc', w_s, vg) + b_s.T[:, :, None]
    return (u * vm.reshape(Bsz, n, D)) @ w_out


def _chunk_mlp(h, hc, w_in, ln_g, ln_b, w_s, b_s, w_out, ctx_out):
    y_lat = _chunk_mlp_seq(h, w_in, ln_g, ln_b, w_s, b_s, w_out)
    y_ctx = _chunk_mlp_seq(hc, w_in, ln_g, ln_b, w_s, b_s, w_out) if ctx_out else None
    return y_lat, y_ctx


def _neighbourhood_attention(h, hc, w_qkv, w_o, rpb, ctx_out):
    Bsz, L, D = h.shape
    rows = L // GRID_W
    kh = min(WIN_H, rows)
    f32 = jnp.float32

    def project(t):
        q, k, v = jnp.split(t @ w_qkv, 3, axis=-1)
        sh = (Bsz, t.shape[1], NA_HEADS, NA_DIM)
        return q.reshape(sh), k.reshape(sh), v.reshape(sh)

    q, k, v = project(h)
    qc, kc, vc = project(hc)
    scale = NA_DIM ** -0.5
    grid = (Bsz, rows, GRID_W, NA_HEADS, NA_DIM)
    qg, kg, vg = q.reshape(grid), k.reshape(grid), v.reshape(grid)
    colq = jnp.arange(GRID_W)
    col_idx = jnp.clip(colq - WIN_W // 2, 0, GRID_W - WIN_W)[:, None] + jnp.arange(WIN_W)
    dc = col_idx - colq[:, None] + (WIN_W - 1)
    rpb_f = rpb.astype(f32)
    n_loc = kh * WIN_W

    def row_block(r):
        rs = jnp.clip(r - kh // 2, 0, rows - kh)
        k_sel = lax.dynamic_slice_in_dim(kg, rs, kh, axis=1)[:, :, col_idx]
        v_sel = lax.dynamic_slice_in_dim(vg, rs, kh, axis=1)[:, :, col_idx]
        q_r = lax.dynamic_index_in_dim(qg, r, axis=1, keepdims=False)
        dr = rs + jnp.arange(kh) - r + (WIN_H - 1)
        bias = rpb_f[:, dr[:, None, None], dc[None]].transpose(0, 2, 1, 3)
        s_loc = jnp.einsum('bqhd,biqjhd->bhqij', q_r, k_sel).astype(f32) * scale + bias
        s_ctx = jnp.einsum('bqhd,bkhd->bhqk', q_r, kc).astype(f32) * scale
        s = jnp.concatenate([s_loc.reshape(Bsz, NA_HEADS, GRID_W, n_loc), s_ctx], axis=-1)
        p = jax.nn.softmax(s, axis=-1).astype(v.dtype)
        p_loc = p[..., :n_loc].reshape(Bsz, NA_HEADS, GRID_W, kh, WIN_W)
        return (jnp.einsum('bhqij,biqjhd->bqhd', p_loc, v_sel)
                + jnp.einsum('bhqk,bkhd->bqhd', p[..., n_loc:], vc))

    o = lax.map(row_block, jnp.arange(rows))
    y_lat = o.swapaxes(0, 1).reshape(Bsz, L, D) @ w_o
    y_ctx = None
    if ctx_out:
        s = jnp.einsum('bqhd,bkhd->bhqk', qc, kc).astype(f32) * scale
        p = jax.nn.softmax(s, axis=-1).astype(vc.dtype)
        y_ctx = jnp.einsum('bhqk,bkhd->bqhd', p, vc).reshape(Bsz, hc.shape[1], D) @ w_o
    return y_lat, y_ctx


def setup_inputs(seed: int = 0) -> dict:
    key = jax.random.key(seed)
    keys = jax.random.split(key, 40)

    def nrm(i, shape, std):
        return jax.random.normal(keys[i], shape, jnp.float32) * std

    D, F, R = D_MODEL, FFN_HIDDEN, ADA_RANK
    G, P, GC = SSM_GROUPS, SSM_STATE, SSM_GROUP
    n_a, n_b, n_c, n_d = [len(range(m, DEPTH, N_MIXERS)) for m in range(N_MIXERS)]
    state_n = jnp.arange(P, dtype=jnp.float32)
    return {
        "x": nrm(0, (BATCH, SEQ, D), 1.0),
        "c": nrm(1, (BATCH, D), 1.0),
        "ctx": nrm(2, (BATCH, CTX_LEN, D), 1.0),
        "c_ctx": nrm(3, (D,), 1.0),
        "ada_down": nrm(4, (DEPTH, D, R), D ** -0.5),
        "ada_up": nrm(5, (DEPTH, R, 6 * D), 0.5 * R ** -0.5),
        "ada_b": nrm(6, (DEPTH, 6 * D), 0.01),
        "ln_g": 1.0 + nrm(7, (DEPTH, 2, D), 0.01),
        "ln_b": nrm(8, (DEPTH, 2, D), 0.01),
        "ffn_w_in": nrm(9, (DEPTH, D, 2 * F), D ** -0.5),
        "ffn_w_out": nrm(10, (DEPTH, F, D), BETA * F ** -0.5),
        "ssm_a_re": -0.5 + nrm(11, (n_a, 2, G, P), 0.01),
        "ssm_a_im": math.pi * state_n + nrm(12, (n_a, 2, G, P), 0.01),
        "ssm_log_dt": jax.random.uniform(keys[13], (n_a, 2, G), jnp.float32,
                                         math.log(DT_MIN), math.log(DT_MAX)),
        "ssm_b_re": nrm(14, (n_a, G, P, GC), (2 * GC) ** -0.5),
        "ssm_b_im": nrm(15, (n_a, G, P, GC), (2 * GC) ** -0.5),
        "ssm_c_re": nrm(16, (n_a, G, GC, P), (2 * P) ** -0.5),
        "ssm_c_im": nrm(17, (n_a, G, GC, P), (2 * P) ** -0.5),
        "ssm_d": nrm(18, (n_a, D), 1.0),
        "ssm_w_glu": nrm(19, (n_a, D, 2 * D), BETA * D ** -0.5),
        "da_w_qkv": nrm(20, (n_b, D, 3 * D), D ** -0.5),
        "da_w_o": nrm(21, (n_b, D, D), BETA * D ** -0.5),
        "da_lambda": nrm(22, (n_b, 4, DA_DIM), 0.1),
        "da_subln_g": 1.0 + nrm(23, (n_b, 2 * DA_DIM), 0.01),
        "cm_w_in": nrm(24, (n_c, D, 2 * D), D ** -0.5),
        "cm_ln_g": 1.0 + nrm(25, (n_c, D), 0.01),
        "cm_ln_b": nrm(26, (n_c, D), 0.01),
        "cm_w_s": nrm(27, (n_c, CM_GROUPS, CM_CHUNK, CM_CHUNK), CM_CHUNK ** -0.5),
        "cm_b_s": 1.0 + nrm(28, (n_c, CM_GROUPS, CM_CHUNK), 0.01),
        "cm_w_out": nrm(29, (n_c, D, D), BETA * D ** -0.5),
        "na_w_qkv": nrm(30, (n_d, D, 3 * D), D ** -0.5),
        "na_w_o": nrm(31, (n_d, D, D), BETA * D ** -0.5),
        "na_rpb": nrm(32, (n_d, NA_HEADS, 2 * WIN_H - 1, 2 * WIN_W - 1), 0.02),
    }


def reference(x, c, ctx, c_ctx, ada_down, ada_up, ada_b, ln_g, ln_b, ffn_w_in, ffn_w_out,
              ssm_a_re, ssm_a_im, ssm_log_dt, ssm_b_re, ssm_b_im, ssm_c_re, ssm_c_im, ssm_d,
              ssm_w_glu, da_w_qkv, da_w_o, da_lambda, da_subln_g,
              cm_w_in, cm_ln_g, cm_ln_b, cm_w_s, cm_b_s, cm_w_out,
              na_w_qkv, na_w_o, na_rpb):
    L = x.shape[1]
    t = jnp.arange(L)
    row, col = t // GRID_W, t % GRID_W
    xc = ctx
    for i in range(DEPTH):
        m, j = i % N_MIXERS, i // N_MIXERS
        ctx_out = i < DEPTH - 1
        sh1, sc1, g1, sh2, sc2, g2 = jnp.split(_adaln(c, ada_down[i], ada_up[i], ada_b[i]), 6, axis=-1)
        shc1, scc1, gc1, shc2, scc2, gc2 = jnp.split(
            _adaln(c_ctx[None], ada_down[i], ada_up[i], ada_b[i]), 6, axis=-1)
        h = x * (1.0 + sc1) + sh1
        hc = xc * (1.0 + scc1) + shc1
        if m == 0:
            y, yc = _s5_mixer(h, hc, ssm_a_re[j], ssm_a_im[j], ssm_log_dt[j], ssm_b_re[j],
                              ssm_b_im[j], ssm_c_re[j], ssm_c_im[j], ssm_d[j], ssm_w_glu[j], ctx_out)
        elif m == 1:
            lam_init = 0.8 - 0.6 * math.exp(-0.3 * i)
            y, yc = _diff_attention(h, hc, row, col, da_w_qkv[j], da_w_o[j], da_lambda[j],
                                    da_subln_g[j], lam_init, ctx_out)
        elif m == 2:
            y, yc = _chunk_mlp(h, hc, cm_w_in[j], cm_ln_g[j], cm_ln_b[j], cm_w_s[j], cm_b_s[j],
                               cm_w_out[j], ctx_out)
        else:
            y, yc = _neighbourhood_attention(h, hc, na_w_qkv[j], na_w_o[j], na_rpb[j], ctx_out)
        x = _layernorm(ALPHA * x + g1 * y, ln_g[i, 0], ln_b[i, 0])
        x = _layernorm(ALPHA * x + g2 * _swiglu(x * (1.0 + sc2) + sh2, ffn_w_in[i], ffn_w_out[i]),
                       ln_g[i, 1], ln_b[i, 1])
        if ctx_out:
            xc = _layernorm(ALPHA * xc + gc1 * yc, ln_g[i, 0], ln_b[i, 0])
            xc = _layernorm(ALPHA * xc + gc2 * _swiglu(xc * (1.0 + scc2) + shc2, ffn_w_in[i],
                                                          ffn_w_out[i]), ln_g[i, 1], ln_b[i, 1])
    return x
```

```python
import contextlib
import math
import numpy as np
import concourse.bass as bass
import concourse.mybir as mybir
from concourse.bass_utils import run_bass_kernel_spmd

F32 = mybir.dt.float32
BF16 = mybir.dt.bfloat16
AF = mybir.ActivationFunctionType
ALU = mybir.AluOpType

D = 4096
KC = 32
L = 4096
CTX = 256
NTOK = CTX + L
F = 11008
FC = 86
DEPTH = 4
GRID = 64
ALPHA = (2 * DEPTH) ** 0.25
EPS = 1e-5
NT = 512
TILES = [(0, CTX, 1)] + [(CTX + NT * i, NT, 0) for i in range(L // NT)]
LAT_TILES = TILES[1:]


class Res:
    __slots__ = ("name", "w", "rs", "t")

    def __init__(self, name, t=None):
        self.name = name
        self.w = None
        self.rs = []
        self.t = t


class Sched:
    def __init__(self, nc, stack, n_dma_sems=14):
        self.nc = nc
        self.stack = stack
        self.E = {"pe": nc.tensor, "act": nc.scalar, "dve": nc.vector, "pool": nc.gpsimd, "sp": nc.sync}
        self.csem = {}
        self.cnt = {}
        for e in ("pe", "act", "dve", "pool"):
            self.csem[e] = stack.enter_context(nc.semaphore("c_" + e))
            self.cnt[e] = 0
        self.seen = {e: {} for e in self.E}
        self.dsems = {}
        self.didx = {}
        self.dval = {}
        for q in ("sp", "act", "pool"):
            self.dsems[q] = [stack.enter_context(nc.semaphore("d_%s%d" % (q, i))) for i in range(n_dma_sems)]
            self.didx[q] = 0
        self.ninst = 0
        self.uid = 0

    def sb(self, name, shape, dtype, stack=None):
        self.uid += 1
        t = (stack or self.stack).enter_context(self.nc.sbuf_tensor("%s_%d" % (name, self.uid), shape, dtype))
        return Res(name, t)

    def ps(self, name, shape, dtype=F32):
        t = self.stack.enter_context(self.nc.psum_tensor(name, shape, dtype))
        return Res(name, t)

    def _wait(self, e, tickets):
        need = {}
        seen = self.seen[e]
        for t in tickets:
            if t is None:
                continue
            sem, val = t
            if seen.get(sem, 0) >= val:
                continue
            if need.get(sem, 0) < val:
                need[sem] = val
        eng = self.E[e]
        for sem, val in need.items():
            eng.wait_ge(sem, val)
            seen[sem] = val
            self.ninst += 1

    def _deps(self, e, reads, writes):
        own = self.csem.get(e)
        deps = []
        for r in reads:
            t = r.w
            if t is not None:
                if t[0] is own and e == "pe":
                    continue
                deps.append(t)
        for w in writes:
            t = w.w
            if t is not None and t[0] is not own:
                deps.append(t)
            for t in w.rs:
                if t[0] is not own:
                    deps.append(t)
        return deps

    def _commit(self, ticket, reads, writes):
        for r in reads:
            r.rs.append(ticket)
            if len(r.rs) > 48:
                best = {}
                for s, v in r.rs:
                    if best.get(s, 0) < v:
                        best[s] = v
                r.rs = list(best.items())
        for w in writes:
            w.w = ticket
            w.rs = []

    def op(self, e, fn, reads=(), writes=(), inc=True):
        self._wait(e, self._deps(e, reads, writes))
        ins = fn(self.E[e])
        self.ninst += 1
        if inc:
            self.cnt[e] += 1
            ins.then_inc(self.csem[e], 1)
            ticket = (self.csem[e], self.cnt[e])
        else:
            ticket = (self.csem[e], self.cnt[e] + 1)
        self._commit(ticket, reads, writes)
        return ticket

    def mm(self, out, lhsT, rhs, start, stop, reads, writes, inc=True):
        return self.op("pe", lambda e: e.matmul(out, lhsT, rhs, start=start, stop=stop), reads, writes, inc)

    def act(self, out, in_, func, reads, writes, **kw):
        return self.op("act", lambda e: e.activation(out, in_, func, **kw), reads, writes)

    def dve(self, fn, reads, writes):
        return self.op("dve", fn, reads, writes)

    def dma(self, q, out, in_, reads=(), writes=()):
        i = self.didx[q]
        self.didx[q] = (i + 1) % len(self.dsems[q])
        sem = self.dsems[q][i]
        prev = self.dval.get(sem, 0)
        deps = self._deps(q, reads, writes)
        if prev:
            deps.append((sem, prev))
        self._wait(q, deps)
        self.E[q].dma_start(out=out, in_=in_).then_inc(sem, 16)
        self.ninst += 1
        self.dval[sem] = prev + 16
        ticket = (sem, prev + 16)
        self._commit(ticket, reads, writes)
        return ticket

    def barrier(self):
        tickets = [(s, v) for s, v in self.dval.items()]
        tickets += [(self.csem[e], self.cnt[e]) for e in self.csem if self.cnt[e] > 0]
        for e in ("sp", "act", "dve", "pool", "pe"):
            self._wait(e, [t for t in tickets if not (e in self.csem and t[0] is self.csem[e])])


def pipelined(n, load, compute, depth):
    for j in range(n + depth):
        if j < n:
            load(j)
        if j >= depth:
            compute(j - depth)


class Prog:
    def __init__(self, layers, tiles=None, final_out=True):
        self.layers = list(layers)
        self.tiles = tiles if tiles is not None else TILES
        self.nc = bass.Bass("TRN2", target_bir_lowering=False)
        self.in_names = []
        self.dres = {}
        self.final_out = final_out

    def din(self, name, shape, dtype=F32):
        self.in_names.append(name)
        return self.nc.dram_tensor(name, list(shape), dtype, kind="ExternalInput").ap()

    def dtmp(self, name, shape, dtype=F32):
        return self.nc.dram_tensor(name, list(shape), dtype, kind="Internal").ap()

    def dr(self, name, key=0):
        k = (name, key)
        if k not in self.dres:
            self.dres[k] = Res("%s_%s" % (name, key))
        return self.dres[k]

    def drall(self, name):
        return [self.dr(name, t[0]) for t in TILES]

    def build(self):
        nc = self.nc
        with contextlib.ExitStack() as st:
            S = self.S = Sched(nc, st)
            self.x_in = self.din("xT", [KC, 128, NTOK])
            self.cvT = self.din("cvT", [128, KC, 2])
            self.lngT = self.din("lngT", [128, DEPTH * 2 * KC])
            self.lnbT = self.din("lnbT", [128, DEPTH * 2 * KC])
            self.W = {}
            for l in self.layers:
                self.W[l] = {
                    "down": self.din("ada_down%d" % l, [128, KC, 256]),
                    "up": self.din("ada_up%d" % l, [128, 2, 6 * D]),
                    "b": self.din("ada_b%d" % l, [128, 192]),
                    "w_in": self.din("ffn_in%d" % l, [FC, 128, 2 * KC * 128]),
                    "w_out": self.din("ffn_out%d" % l, [KC, 128, FC * 128]),
                }
            self.out = nc.dram_tensor("outT", [KC, 128, L], F32, kind="ExternalOutput").ap()
            self.xT = [self.dtmp("xTa", [KC, 128, NTOK]), self.dtmp("xTb", [KC, 128, NTOK])]
            self.rT = self.dtmp("rT", [KC, 128, NTOK])
            self.xmid = self.dtmp("xmid", [KC, 128, NTOK])
            self.ps = [S.ps("ps%d" % i, [128, 512]) for i in range(8)]
            self.modv = {l: S.sb("modv%d" % l, [128, 6, KC, 2], F32) for l in self.layers}
            self.lng = S.sb("lng", [128, DEPTH * 2 * KC], F32)
            self.lnb = S.sb("lnb", [128, DEPTH * 2 * KC], F32)
            self.ones_f = S.sb("ones_f", [128, 128], F32)
            self.ones_b = S.sb("ones_b", [128, 128], BF16)
            self.mean_t = S.sb("mean_t", [128, 512], F32)
            self.rstd_t = S.sb("rstd_t", [128, 512], F32)
            self.eps_t = S.sb("eps_t", [128, 1], F32)
            S.dve(lambda e: e.memset(self.eps_t.t[:], EPS), [], [self.eps_t])
            S.dve(lambda e: e.memset(self.ones_f.t[:], 1.0), [], [self.ones_f])
            S.dve(lambda e: e.memset(self.ones_b.t[:], 1.0), [], [self.ones_b])
            S.dma("sp", self.lng.t[:], self.lngT, writes=[self.lng])
            S.dma("sp", self.lnb.t[:], self.lnbT, writes=[self.lnb])
            self.declare_layer_inputs()
            self.declare_attn_inputs()
            self.declare_s5_inputs()
            self.adaln()
            cur = self.x_in
            cur_name = "x_in"
            for l in self.layers:
                nxt = self.xT[l % 2]
                nxt_name = "xT%d" % (l % 2)
                self.cur, self.cur_name, self.nxt, self.nxt_name = cur, cur_name, nxt, nxt_name
                self.mid, self.mid_name = self.xmid, "xmid"
                getattr(self, "layer%d" % l)(l)
                cur, cur_name = nxt, nxt_name
            with contextlib.ExitStack() as ls:
                bufs = [S.sb("ob", [128, 8, 512], F32, ls) for _ in range(2)]
                i = 0
                for kc0 in range(0, KC, 8):
                    for (c0, n, r) in LAT_TILES:
                        if (c0, n, r) not in self.tiles:
                            continue
                        b = bufs[i % 2]
                        i += 1
                        S.dma("sp", b.t[:, :, :n], cur[kc0:kc0 + 8, :, c0:c0 + n].rearrange("k p t -> p k t"),
                              reads=[self.dr(cur_name, c0)], writes=[b])
                        S.dma("act", self.out[kc0:kc0 + 8, :, c0 - CTX:c0 - CTX + n].rearrange("k p t -> p k t"), b.t[:, :, :n],
                              reads=[b])
            S.barrier()
            self.ninst = S.ninst
        return nc

    def declare_layer_inputs(self):
        if 2 in self.layers:
            self.cm = {
                "w_u": self.din("cm_wu", [KC // 2, 128, KC * 256]),
                "w_v": self.din("cm_wv", [D // 256, 128, KC * 256]),
                "w_out": self.din("cm_wout", [KC // 2, 128, KC * 256]),
                "lngb": self.din("cm_lngb", [128, 2 * KC]),
                "wsT": self.din("cm_wsT", [128, 16 * 128]),
                "bs": self.din("cm_bs", [1, 16 * 128]),
            }

    def declare_attn_inputs(self):
        if 1 in self.layers or 3 in self.layers:
            self.qkT = self.dtmp("qkT", [64, 128, NTOK], BF16)
            self.Vd = self.dtmp("Vd", [NTOK, D], BF16)
            self.oT = self.dtmp("oT", [KC, 128, NTOK], BF16)
        if 1 in self.layers:
            self.da = {
                "w_qk": self.din("da_wqk", [32, 128, KC * 256]),
                "w_v": self.din("da_wv", [16, 128, KC * 256]),
                "w_o": self.din("da_wo", [16, 128, KC * 256]),
                "lam": self.din("da_lam", [128, 4]),
                "subln": self.din("da_subln", [128, 2]),
                "ropeC": self.din("ropeC", [128, L]),
                "ropeS": self.din("ropeS", [128, L]),
                "perm": self.din("ropeP", [128, 128]),
            }
        if 3 in self.layers:
            self.na = {
                "w_qk": self.din("na_wqk", [32, 128, KC * 256]),
                "w_v": self.din("na_wv", [16, 128, KC * 256]),
                "w_o": self.din("na_wo", [16, 128, KC * 256]),
                "bias": self.din("na_bias", [32, 128, 15 * 64]),
                "mask": self.din("na_mask", [128, 15 * 64]),
            }

    def qkv_project(self, l, w, rope):
        S = self.S
        with contextlib.ExitStack() as lsC:
            if rope:
                perm = S.sb("perm", [128, 128], F32, lsC)
                S.dma("sp", perm.t[:], self.da["perm"], writes=[perm])
            for tile in self.tiles_all:
                c0, n, r = tile
                nch = n // 128
                with contextlib.ExitStack() as ls0:
                    hT = S.sb("hT", [128, KC, 512], BF16, ls0)
                    with contextlib.ExitStack() as ls:
                        self.mod_to_hT(l, 0, self.cur, self.cur_name, tile, hT, ls)
                    S.barrier()
                    with contextlib.ExitStack() as ls:
                        wb = [S.sb("wqk", [128, KC * 256], BF16, ls) for _ in range(3)]
                        qo = [S.sb("qo", [128, 512], BF16, ls) for _ in range(3)]
                        do_rope = rope and r == 0
                        if do_rope:
                            cs = S.sb("ropec", [128, 512], F32, ls)
                            sn = S.sb("ropes", [128, 512], F32, ls)
                            S.dma("sp", cs.t[:, :n], self.da["ropeC"][:, c0 - CTX:c0 - CTX + n], writes=[cs])
                            S.dma("sp", sn.t[:, :n], self.da["ropeS"][:, c0 - CTX:c0 - CTX + n], writes=[sn])
                            q32 = [S.sb("q32", [128, 512], F32, ls) for _ in range(2)]
                            t1 = [S.sb("t1", [128, 512], F32, ls) for _ in range(2)]
                            t2 = [S.sb("t2r", [128, 512], F32, ls) for _ in range(2)]

                        def epi(c, ps):
                            o = qo[c % 3]
                            if do_rope:
                                a, b1, b2 = q32[c % 2], t1[c % 2], t2[c % 2]
                                rp = self.ps[4 + c % 2]
                                S.act(a.t[:, :n], ps.t[:, :n], AF.Copy, [ps], [a])
                                S.mm(rp.t[:, :n], perm.t[:], a.t[:, :n], start=True, stop=True, reads=[perm, a], writes=[rp])
                                S.dve(lambda e: e.tensor_tensor(b1.t[:, :n], a.t[:, :n], cs.t[:, :n], ALU.mult), [a, cs], [b1])
                                S.dve(lambda e: e.tensor_tensor(b2.t[:, :n], rp.t[:, :n], sn.t[:, :n], ALU.mult), [rp, sn], [b2])
                                S.dve(lambda e: e.tensor_tensor(o.t[:, :n], b1.t[:, :n], b2.t[:, :n], ALU.add), [b1, b2], [o])
                            else:
                                S.act(o.t[:, :n], ps.t[:, :n], AF.Copy, [ps], [o])
                            S.dma("act", self.qkT[c, :, c0:c0 + n], o.t[:, :n], reads=[o], writes=[self.dr("qkT", c0)])

                        self.gemm_fm(w["w_qk"], 32, KC, 256, hT, n, wb, self.ps[0:4], epi)
                        vo = [S.sb("vo", [128, 256], BF16, ls) for _ in range(3)]
                        cnt = [0]

                        fresh = {}

                        def load(b):
                            fresh[b] = self.wload(w["w_v"], b, wb[b % 3], KC * 256)

                        def comp(b):
                            wt = wb[b % 3]
                            if fresh[b]:
                                self.wstore(w["w_v"], b, wt, KC * 256)
                            for ch in range(nch):
                                ps = self.ps[cnt[0] % 4]
                                o = vo[cnt[0] % 3]
                                cnt[0] += 1
                                for kc in range(KC):
                                    S.mm(ps.t[:, :256], hT.t[:, kc, ch * 128:(ch + 1) * 128], wt.t[:, kc * 256:(kc + 1) * 256],
                                         start=(kc == 0), stop=(kc == KC - 1), reads=[wt, hT], writes=[ps], inc=(kc == KC - 1))
                                S.act(o.t[:], ps.t[:, :256], AF.Copy, [ps], [o])
                                S.dma("act", self.Vd[c0 + ch * 128:c0 + (ch + 1) * 128, b * 256:(b + 1) * 256], o.t[:], reads=[o],
                                      writes=[self.dr("Vd", c0)])

                        pipelined(16, load, comp, 2)
                S.barrier()

    def attn_out_tail(self, l, w, tiles):
        S = self.S
        for tile in tiles:
            c0, n, r = tile
            with contextlib.ExitStack() as ls:
                oin = S.sb("oin", [128, KC, 512], BF16, ls)
                S.dma("sp", oin.t[:, :, :n], self.oT[:, :, c0:c0 + n].rearrange("k p t -> p k t"), reads=[self.dr("oT", c0)], writes=[oin])
                wb = [S.sb("wo", [128, KC * 256], BF16, ls) for _ in range(3)]
                bufs = [[S.sb("e%d" % q, [128, 512], F32, ls) for q in range(4)] for _ in range(2)]
                stats = [self.ps[6], self.ps[7]]

                def epi_o(c, ps):
                    self.resid_epilogue(l, 0, tile, c, ps.t[:, :n], ps, self.cur, self.cur_name, bufs[c % 2], stats, c == 0, c == KC - 1)

                self.gemm_fm(w["w_o"], 16, KC, 256, oin, n, wb, self.ps[0:4], epi_o)
                self.ln_finish_stats(n, stats)
            S.barrier()
            self.ffn_tail(l, tile)

    def layer3(self, l):
        S = self.S
        na = self.na
        self.tiles_all = TILES
        self.qkv_project(l, na, rope=False)
        scale = 128 ** -0.5
        with contextlib.ExitStack() as ls:
            mask = S.sb("namask", [128, 15, 64], F32, ls)
            S.dma("sp", mask.t[:].rearrange("p a q -> p (a q)"), na["mask"], writes=[mask])
            kT = [S.sb("nkT", [128, NTOK], BF16, ls) for _ in range(2)]
            qT = [S.sb("nqT", [128, L], BF16, ls) for _ in range(2)]
            Ve = [S.sb("nVe", [128, 34, 128], BF16, ls) for _ in range(2)]
            Vo = [S.sb("nVo", [128, 33, 128], BF16, ls) for _ in range(2)]
            TT = [S.sb("nTT", [128, 15, 64], F32, ls) for _ in range(2)]
            tmp = [S.sb("ntmp", [128, 256], F32, ls) for _ in range(3)]
            Eb = [S.sb("nE", [128, 384], BF16, ls) for _ in range(3)]
            rz = [S.sb("nrz", [128, 512], F32, ls) for _ in range(2)]
            oh = [S.sb("noh", [128, 512], BF16, ls) for _ in range(2)]
            qkall = [self.dr("qkT", t[0]) for t in TILES]
            vall = [self.dr("Vd", t[0]) for t in TILES]
            rows_needed = sorted(set((t[0] - CTX) // 64 + i for t in self.tiles if t[2] == 0 for i in range(t[1] // 64)))
            it = 0
            g8 = 0

            def load(h):
                i = h % 2
                S.dma("sp", kT[i].t[:], self.qkT[32 + h], reads=qkall, writes=[kT[i]])
                S.dma("sp", qT[i].t[:], self.qkT[h, :, CTX:], reads=qkall, writes=[qT[i]])
                S.dma("sp", Ve[i].t[:], self.Vd[:, h * 128:(h + 1) * 128].rearrange("(c p) e -> p c e", p=128), reads=vall, writes=[Ve[i]])
                S.dma("sp", Vo[i].t[:], self.Vd[64:64 + 33 * 128, h * 128:(h + 1) * 128].rearrange("(c p) e -> p c e", p=128), reads=vall, writes=[Vo[i]])
                S.dma("sp", TT[i].t[:].rearrange("p a q -> p (a q)"), na["bias"][h], writes=[TT[i]])
                S.op("pool", lambda e: e.tensor_tensor(TT[i].t[:], TT[i].t[:], mask.t[:], ALU.add), [TT[i], mask], [TT[i]])

            def comp(h):
                nonlocal it, g8
                i = h % 2
                k_, q_, ve, vo_, tt = kT[i], qT[i], Ve[i], Vo[i], TT[i]
                for r0 in range(0, GRID, 8):
                    rows = [r for r in range(r0, r0 + 8) if r in rows_needed]
                    if not rows:
                        continue
                    ops_, zps = self.ps[2 + g8 % 2], self.ps[4 + g8 % 2]
                    for r in rows:
                        rs = min(max(r - 4, 0), GRID - 8)
                        a0 = rs - r + 7
                        sp_ = self.ps[it % 2]
                        tm, E = tmp[it % 3], Eb[it % 3]
                        it += 1
                        q_ap = q_.t[:, 64 * r:64 * r + 64]
                        for j in range(6):
                            kc0 = CTX + 64 * (rs + 2 * j) if j < 4 else 128 * (j - 4)
                            S.mm(sp_.t[:, j * 64:(j + 1) * 64], k_.t[:, kc0:kc0 + 128], q_ap, start=True, stop=True,
                                 reads=[k_, q_], writes=[sp_], inc=(j == 5))
                        S.dve(lambda e: e.scalar_tensor_tensor(tm.t[:].rearrange("p (j q) -> p j q", q=64),
                                                               sp_.t[:, 0:256].rearrange("p (j q) -> p j q", q=64), scale,
                                                               tt.t[:, a0:a0 + 7:2, :], ALU.mult, ALU.add), [sp_, tt], [tm])
                        S.act(E.t[:, 0:256], tm.t[:], AF.Exp, [tm], [E])
                        S.act(E.t[:, 256:384], sp_.t[:, 256:384], AF.Exp, [sp_], [E], scale=scale)
                        co = (r - r0) * 64
                        for j in range(6):
                            if j < 4:
                                kr = rs + 2 * j
                                v_ap = ve.t[:, 2 + kr // 2, :] if rs % 2 == 0 else vo_.t[:, (3 + kr) // 2, :]
                            else:
                                v_ap = ve.t[:, j - 4, :]
                            S.mm(ops_.t[:, co:co + 64], v_ap, E.t[:, j * 64:(j + 1) * 64], start=(j == 0), stop=(j == 5),
                                 reads=[ve, vo_, E], writes=[ops_], inc=False)
                        for j in range(6):
                            S.mm(zps.t[:, co:co + 64], self.ones_b.t[:], E.t[:, j * 64:(j + 1) * 64], start=(j == 0), stop=(j == 5),
                                 reads=[self.ones_b, E], writes=[zps], inc=(j == 5))
                    c_lo, c_hi = (rows[0] - r0) * 64, (rows[-1] - r0 + 1) * 64
                    z_, o_ = rz[g8 % 2], oh[g8 % 2]
                    g8 += 1
                    S.dve(lambda e: e.reciprocal(z_.t[:, c_lo:c_hi], zps.t[:, c_lo:c_hi]), [zps], [z_])
                    S.dve(lambda e: e.tensor_tensor(o_.t[:, c_lo:c_hi], ops_.t[:, c_lo:c_hi], z_.t[:, c_lo:c_hi], ALU.mult), [ops_, z_], [o_])
                    tcol = CTX + 64 * r0
                    S.dma("act", self.oT[h, :, tcol + c_lo:tcol + c_hi], o_.t[:, c_lo:c_hi], reads=[o_],
                          writes=[self.dr("oT", CTX + NT * (r0 // 8))])

            pipelined(32, load, comp, 1)
        S.barrier()
        self.attn_out_tail(l, na, [t for t in self.tiles if t[2] == 0])

    def layer1(self, l):
        S = self.S
        da = self.da
        self.tiles_all = TILES
        self.qkv_project(l, da, rope=True)
        scale = 128 ** -0.5
        lam_init = 0.8 - 0.6 * math.exp(-0.3 * l)
        with contextlib.ExitStack() as ls:
            lp = S.sb("lp", [128, 4], F32, ls)
            pr = S.sb("pr", [128, 2], F32, ls)
            ee = S.sb("ee", [128, 2], F32, ls)
            neglam = S.sb("neglam", [128, 1], F32, ls)
            gsc = S.sb("gsc", [128, 2], F32, ls)
            S.dma("sp", lp.t[:], da["lam"], writes=[lp])
            S.dma("sp", gsc.t[:], da["subln"], writes=[gsc])
            S.dve(lambda e: e.tensor_tensor(pr.t[:, 0:1], lp.t[:, 0:1], lp.t[:, 1:2], ALU.mult), [lp], [pr])
            S.dve(lambda e: e.tensor_tensor(pr.t[:, 1:2], lp.t[:, 2:3], lp.t[:, 3:4], ALU.mult), [lp], [pr])
            S.mm(self.ps[0].t[:, 0:2], self.ones_f.t[:], pr.t[:], start=True, stop=True, reads=[pr, self.ones_f], writes=[self.ps[0]])
            S.act(ee.t[:], self.ps[0].t[:, 0:2], AF.Exp, [self.ps[0]], [ee])
            S.dve(lambda e: e.scalar_tensor_tensor(neglam.t[:], ee.t[:, 1:2], -lam_init, ee.t[:, 0:1], ALU.add, ALU.subtract), [ee], [neglam])
            S.dve(lambda e: e.tensor_scalar_mul(gsc.t[:], gsc.t[:], 1.0 - lam_init), [gsc], [gsc])
            kT = [S.sb("dkT", [128, 2, NTOK], BF16, ls) for _ in range(2)]
            qT = [S.sb("dqT", [128, 2, NTOK], BF16, ls) for _ in range(2)]
            Vh = [S.sb("dVh", [128, 34, 256], BF16, ls) for _ in range(2)]
            Eb = [S.sb("dE", [128, 512], BF16, ls) for _ in range(3)]
            rz = S.sb("drz", [128, 512], F32, ls)
            On = [S.sb("dOn", [128, 2, 512], F32, ls) for _ in range(2)]
            comb = S.sb("dcomb", [128, 2, 512], F32, ls)
            sq = S.sb("dsq", [128, 2, 512], F32, ls)
            rr = S.sb("drr", [128, 512], F32, ls)
            ob = [S.sb("dob", [128, 2, 512], BF16, ls) for _ in range(2)]
            qkall = [self.dr("qkT", t[0]) for t in TILES]
            vall = [self.dr("Vd", t[0]) for t in TILES]
            it = 0
            nq = 0

            def load(h):
                i = h % 2
                S.dma("sp", kT[i].t[:], self.qkT[32 + 2 * h:34 + 2 * h].rearrange("c p t -> p c t"), reads=qkall, writes=[kT[i]])
                S.dma("sp", qT[i].t[:], self.qkT[2 * h:2 * h + 2].rearrange("c p t -> p c t"), reads=qkall, writes=[qT[i]])
                S.dma("sp", Vh[i].t[:], self.Vd[:, h * 256:(h + 1) * 256].rearrange("(c p) e -> p c e", p=128), reads=vall, writes=[Vh[i]])

            def comp(h):
                nonlocal it, nq
                i = h % 2
                k_, q_, v_ = kT[i], qT[i], Vh[i]
                for tile in self.tiles:
                    c0, n, r = tile
                    kcs = list(range(2)) if r == 1 else list(range(34))
                    for sub in range(2):
                        O0, O1, Z = self.ps[2 + 3 * sub], self.ps[3 + 3 * sub], self.ps[4 + 3 * sub]
                        for ki, kc in enumerate(kcs):
                            sp_ = self.ps[it % 2]
                            E = Eb[it % 3]
                            it += 1
                            S.mm(sp_.t[:, :n], k_.t[:, sub, kc * 128:(kc + 1) * 128], q_.t[:, sub, c0:c0 + n], start=True, stop=True,
                                 reads=[k_, q_], writes=[sp_])
                            S.act(E.t[:, :n], sp_.t[:, :n], AF.Exp, [sp_], [E], scale=scale)
                            first, last = ki == 0, ki == len(kcs) - 1
                            S.mm(O0.t[:, :n], v_.t[:, kc, 0:128], E.t[:, :n], start=first, stop=last, reads=[v_, E], writes=[O0], inc=False)
                            S.mm(O1.t[:, :n], v_.t[:, kc, 128:256], E.t[:, :n], start=first, stop=last, reads=[v_, E], writes=[O1], inc=False)
                            S.mm(Z.t[:, :n], self.ones_b.t[:], E.t[:, :n], start=first, stop=last, reads=[self.ones_b, E], writes=[Z], inc=True)
                        S.dve(lambda e: e.reciprocal(rz.t[:, :n], Z.t[:, :n]), [Z], [rz])
                        S.dve(lambda e: e.tensor_tensor(On[sub].t[:, 0, :n], O0.t[:, :n], rz.t[:, :n], ALU.mult), [O0, rz], [On[sub]])
                        S.dve(lambda e: e.tensor_tensor(On[sub].t[:, 1, :n], O1.t[:, :n], rz.t[:, :n], ALU.mult), [O1, rz], [On[sub]])
                    S.dve(lambda e: e.scalar_tensor_tensor(comb.t[:, :, :n], On[1].t[:, :, :n], neglam.t[:, 0:1], On[0].t[:, :, :n], ALU.mult, ALU.add),
                          [On[0], On[1], neglam], [comb])
                    S.act(sq.t[:, :, :n], comb.t[:, :, :n], AF.Square, [comb], [sq])
                    mp = self.ps[4]
                    for ec in range(2):
                        S.mm(mp.t[:, :n], self.ones_f.t[:], sq.t[:, ec, :n], start=(ec == 0), stop=(ec == 1), reads=[sq, self.ones_f], writes=[mp], inc=(ec == 1))
                    S.act(rr.t[:, :n], mp.t[:, :n], AF.Sqrt, [mp], [rr], scale=1.0 / 256.0, bias=self.eps_t.t[:, 0:1])
                    S.dve(lambda e: e.reciprocal(rr.t[:, :n], rr.t[:, :n]), [rr], [rr])
                    o_ = ob[nq % 2]
                    nq += 1
                    for ec in range(2):
                        S.dve(lambda e: e.scalar_tensor_tensor(o_.t[:, ec, :n], comb.t[:, ec, :n], gsc.t[:, ec:ec + 1], rr.t[:, :n], ALU.mult, ALU.mult),
                              [comb, gsc, rr], [o_])
                    S.dma("act", self.oT[2 * h:2 * h + 2, :, c0:c0 + n].rearrange("c p t -> p c t"), o_.t[:, :, :n], reads=[o_],
                          writes=[self.dr("oT", c0)])

            pipelined(16, load, comp, 1)
        S.barrier()
        self.attn_out_tail(l, da, self.tiles)

    def declare_s5_inputs(self):
        if 0 in self.layers:
            self.s5 = {
                "par": self.din("s5_par", [128, 2 * 3 * 128]),
                "BT": self.din("s5_BT", [128, 32, 2 * 128]),
                "CT": self.din("s5_CT", [128, 2 * 128 * 32]),
                "d": self.din("s5_d", [128, KC]),
                "w_glu": self.din("s5_wglu", [KC, 128, 2 * KC * 128]),
            }
            self.hbf = self.dtmp("hbf", [KC, 128, NTOK], BF16)
            self.yT = self.dtmp("yT", [KC, 128, NTOK])

    def layer0(self, l):
        S = self.S
        s5 = self.s5
        PI = math.pi
        for tile in TILES:
            c0, n, r = tile
            with contextlib.ExitStack() as ls:
                hT = S.sb("hT", [128, KC, 512], BF16, ls)
                self.mod_to_hT(l, 0, self.cur, self.cur_name, tile, hT, ls)
                S.dma("act", self.hbf[:, :, c0:c0 + n].rearrange("k p t -> p k t"), hT.t[:, :, :n], reads=[hT], writes=[self.dr("hbf", c0)])
            S.barrier()
        hall = [self.dr("hbf", t[0]) for t in TILES]
        with contextlib.ExitStack() as lsP:
            par = S.sb("par", [128, 2, 3, 128], F32, lsP)
            S.dma("sp", par.t[:].rearrange("p a b c -> p (a b c)"), s5["par"], writes=[par])
            rr_ = S.sb("s5r", [128, 2, 128], F32, lsP)
            CP = S.sb("s5CP", [128, 2, 11, 2, 128], F32, lsP)
            cf = S.sb("s5cf", [128, 2, 2, 128], F32, lsP)
            CT = S.sb("s5CT", [128, 2, 128, 32], BF16, lsP)
            negpi = S.sb("negpi", [128, 1], F32, lsP)
            S.dve(lambda e: e.memset(negpi.t[:], -PI), [], [negpi])
            S.dma("pool", CT.t[:].rearrange("p a b c -> p (a b c)"), s5["CT"], writes=[CT])
            S.op("pool", lambda e: e.tensor_scalar_mul(CT.t[:, 1], CT.t[:, 1], -1.0), [CT], [CT])
            with contextlib.ExitStack() as ls:
                def T_(nm):
                    return S.sb(nm, [128, 2, 128], F32, ls)
                dt, ar, th, a1, a2, cth, sth, abr, abi, den, n1, n2 = [T_("pp%d" % i) for i in range(12)]
                are, aim, ldt = par.t[:, :, 0, :], par.t[:, :, 1, :], par.t[:, :, 2, :]
                S.act(dt.t[:], ldt, AF.Exp, [par], [dt])
                S.dve(lambda e: e.tensor_tensor(ar.t[:], are, dt.t[:], ALU.mult), [par, dt], [ar])
                S.dve(lambda e: e.tensor_tensor(th.t[:], aim, dt.t[:], ALU.mult), [par, dt], [th])
                S.act(rr_.t[:], ar.t[:], AF.Exp, [ar], [rr_])
                for (a_, off) in ((a1, PI), (a2, 1.5 * PI)):
                    S.dve(lambda e: e.tensor_scalar_add(n2.t[:], th.t[:], off), [th], [n2])
                    S.dve(lambda e: e.tensor_copy(a_.t[:], n2.t[:]), [n2], [a_])
                    for m_ in range(1, 6):
                        S.dve(lambda e: e.tensor_scalar(n1.t[:], n2.t[:], 2 * PI * m_, 2 * PI, ALU.is_ge, ALU.mult), [n2], [n1])
                        S.dve(lambda e: e.tensor_tensor(a_.t[:], a_.t[:], n1.t[:], ALU.subtract), [a_, n1], [a_])
                S.act(sth.t[:], a1.t[:], AF.Sin, [a1, negpi], [sth], bias=negpi.t[:, 0:1])
                S.act(cth.t[:], a2.t[:], AF.Sin, [a2, negpi], [cth], bias=negpi.t[:, 0:1])
                S.dve(lambda e: e.tensor_copy(CP.t[:, :, 0, 0, :], cth.t[:]), [cth], [CP])
                S.dve(lambda e: e.tensor_copy(CP.t[:, :, 0, 1, :], sth.t[:]), [sth], [CP])
                for k in range(10):
                    cr, ci = CP.t[:, :, k, 0, :], CP.t[:, :, k, 1, :]
                    S.dve(lambda e: e.tensor_tensor(n1.t[:], cr, cr, ALU.mult), [CP], [n1])
                    S.dve(lambda e: e.tensor_tensor(n2.t[:], ci, ci, ALU.mult), [CP], [n2])
                    S.dve(lambda e: e.tensor_tensor(CP.t[:, :, k + 1, 0, :], n1.t[:], n2.t[:], ALU.subtract), [n1, n2], [CP])
                    S.dve(lambda e: e.scalar_tensor_tensor(CP.t[:, :, k + 1, 1, :], cr, 2.0, ci, ALU.mult, ALU.mult), [CP], [CP])
                S.dve(lambda e: e.tensor_tensor(abr.t[:], rr_.t[:], cth.t[:], ALU.mult), [rr_, cth], [abr])
                S.dve(lambda e: e.tensor_scalar_add(abr.t[:], abr.t[:], -1.0), [abr], [abr])
                S.dve(lambda e: e.tensor_tensor(abi.t[:], rr_.t[:], sth.t[:], ALU.mult), [rr_, sth], [abi])
                S.dve(lambda e: e.tensor_tensor(den.t[:], are, are, ALU.mult), [par], [den])
                S.dve(lambda e: e.tensor_tensor(n1.t[:], aim, aim, ALU.mult), [par], [n1])
                S.dve(lambda e: e.tensor_tensor(den.t[:], den.t[:], n1.t[:], ALU.add), [den, n1], [den])
                S.dve(lambda e: e.reciprocal(den.t[:], den.t[:]), [den], [den])
                S.dve(lambda e: e.tensor_tensor(n1.t[:], abr.t[:], are, ALU.mult), [abr, par], [n1])
                S.dve(lambda e: e.tensor_tensor(n2.t[:], abi.t[:], aim, ALU.mult), [abi, par], [n2])
                S.dve(lambda e: e.tensor_tensor(n1.t[:], n1.t[:], n2.t[:], ALU.add), [n1, n2], [n1])
                S.dve(lambda e: e.tensor_tensor(cf.t[:, :, 0, :], n1.t[:], den.t[:], ALU.mult), [n1, den], [cf])
                S.dve(lambda e: e.tensor_tensor(n1.t[:], abi.t[:], are, ALU.mult), [abi, par], [n1])
                S.dve(lambda e: e.tensor_tensor(n2.t[:], abr.t[:], aim, ALU.mult), [abr, par], [n2])
                S.dve(lambda e: e.tensor_tensor(n1.t[:], n1.t[:], n2.t[:], ALU.subtract), [n1, n2], [n1])
                S.dve(lambda e: e.tensor_tensor(cf.t[:, :, 1, :], n1.t[:], den.t[:], ALU.mult), [n1, den], [cf])
            S.barrier()
            nCre = S.sb("s5nC", [128, 128, 32], BF16, lsP)
            S.op("pool", lambda e: e.tensor_scalar_mul(nCre.t[:], CT.t[:, 0], -1.0), [CT], [nCre])
            segs_f = [(0, CTX)] + [(CTX + 1024 * i, CTX + 1024 * (i + 1)) for i in range(4)]
            segs = {0: segs_f, 1: [(0, CTX)] + segs_f[:0:-1]}
            with contextlib.ExitStack() as ls:
                uT = S.sb("s5u", [32, NTOK], BF16, ls)
                BT = [S.sb("s5B", [32, 2, 128], BF16, ls) for _ in range(2)]
                M = [[S.sb("s5M%d" % p, [128, 1025], F32, ls) for p in range(2)] for _ in range(2)]
                Wz = [[S.sb("s5W%d" % p, [128, 1024], F32, ls) for p in range(2)] for _ in range(2)]
                mt = [S.sb("s5mt", [128, 512], F32, ls) for _ in range(2)]
                Pt = [[S.sb("s5P%d" % p, [128, 512], F32, ls) for p in range(4)] for _ in range(2)]
                Z = [[S.sb("s5Z%d" % p, [128, 1024], F32, ls) for p in range(2)] for _ in range(2)]
                G = [S.sb("s5G%d" % p, [128, 1024], F32, ls) for p in range(2)]
                Q = [[S.sb("s5Q%d" % p, [128, 1024], BF16, ls) for p in range(4)] for _ in range(2)]
                carry = [S.sb("s5c", [128, 2], F32, ls) for _ in range(2)]
                ctmp = S.sb("s5ct", [128, 2], F32, ls)
                yacc = S.sb("s5y", [32, NTOK], F32, ls)
                it = 0
                sg = 0
                cb = 0
                yi = 0

                def load(st):
                    S.dma("pool", BT[st % 2].t[:].rearrange("c a p -> c (a p)"), s5["BT"][st], writes=[BT[st % 2]])

                def tables(st, d_):
                        Mre, Mim = M[d_]
                        Wre, Wim = Wz[d_]
                        def cp(k, part):
                            return CP.t[:, d_, k, part, st:st + 1]
                        S.op("pool", lambda e: e.memset(Mre.t[:, 0:1], 1.0), [], [Mre])
                        S.op("pool", lambda e: e.memset(Mim.t[:, 0:1], 0.0), [], [Mim])
                        ta, tb = mt[0], mt[1]
                        for k in range(10):
                            nn = 1 << k
                            S.act(ta.t[:, :nn], Mre.t[:, 0:nn], AF.Copy, [Mre, CP], [ta], scale=cp(k, 0))
                            S.act(tb.t[:, :nn], Mim.t[:, 0:nn], AF.Copy, [Mim, CP], [tb], scale=cp(k, 1))
                            S.op("pool", lambda e: e.tensor_tensor(Mre.t[:, nn:2 * nn], ta.t[:, :nn], tb.t[:, :nn], ALU.subtract), [ta, tb], [Mre])
                            S.act(ta.t[:, :nn], Mre.t[:, 0:nn], AF.Copy, [Mre, CP], [ta], scale=cp(k, 1))
                            S.act(tb.t[:, :nn], Mim.t[:, 0:nn], AF.Copy, [Mim, CP], [tb], scale=cp(k, 0))
                            S.op("pool", lambda e: e.tensor_tensor(Mim.t[:, nn:2 * nn], ta.t[:, :nn], tb.t[:, :nn], ALU.add), [ta, tb], [Mim])
                        S.act(Mre.t[:, 1024:1025], cp(10, 0), AF.Copy, [CP], [Mre])
                        S.act(Mim.t[:, 1024:1025], cp(10, 1), AF.Copy, [CP], [Mim])
                        cfr, cfi = cf.t[:, d_, 0, st:st + 1], cf.t[:, d_, 1, st:st + 1]
                        for h0 in range(0, 1024, 512):
                            S.act(ta.t[:], Mre.t[:, h0:h0 + 512], AF.Copy, [Mre, cf], [ta], scale=cfr)
                            S.act(tb.t[:], Mim.t[:, h0:h0 + 512], AF.Copy, [Mim, cf], [tb], scale=cfi)
                            S.op("pool", lambda e: e.tensor_tensor(Wre.t[:, h0:h0 + 512], ta.t[:], tb.t[:], ALU.add), [ta, tb], [Wre])
                            S.act(ta.t[:], Mre.t[:, h0:h0 + 512], AF.Copy, [Mre, cf], [ta], scale=cfi)
                            S.act(tb.t[:], Mim.t[:, h0:h0 + 512], AF.Copy, [Mim, cf], [tb], scale=cfr)
                            S.op("pool", lambda e: e.tensor_tensor(Wim.t[:, h0:h0 + 512], ta.t[:], tb.t[:], ALU.subtract), [ta, tb], [Wim])

                def comp(st):
                    nonlocal it, sg, cb, yi
                    u_, b_ = uT, BT[st % 2]
                    S.dma("sp", u_.t[:], self.hbf[st // 4, 32 * (st % 4):32 * (st % 4) + 32, :], reads=hall, writes=[u_])
                    for d_ in range(2):
                        Mre, Mim = M[d_]
                        Wre, Wim = Wz[d_]
                        if d_ == 0:
                            tables(st, 1)
                        elif st + 1 < 128:
                            tables(st + 1, 0)
                        prev_carry = None
                        for (c0, c1) in segs[d_]:
                            Ls = c1 - c0
                            Zre, Zim = Z[sg % 2]
                            Q1, Q2, Q3, Q4 = Q[sg % 2]
                            Gre, Gim = G
                            sg += 1
                            rev = d_ == 1
                            for k0 in range(c0, c1, 512):
                                k1 = min(k0 + 512, c1)
                                w_ = k1 - k0
                                pr_, pi_ = self.ps[(it % 2) * 2], self.ps[(it % 2) * 2 + 1]
                                p1, p2, p3, p4 = Pt[it % 2]
                                it += 1
                                S.mm(pr_.t[:, :w_], b_.t[:, 0, :], u_.t[:, k0:k1], start=True, stop=True, reads=[b_, u_], writes=[pr_])
                                S.mm(pi_.t[:, :w_], b_.t[:, 1, :], u_.t[:, k0:k1], start=True, stop=True, reads=[b_, u_], writes=[pi_])
                                if not rev:
                                    lo, hi = k0 - c0, k1 - c0
                                    wr, wi = Wre.t[:, lo:hi], Wim.t[:, lo:hi]
                                else:
                                    lo, hi = c1 - k1, c1 - k0
                                    wr, wi = Wre.t[:, lo:hi][:, ::-1], Wim.t[:, lo:hi][:, ::-1]
                                zo = slice(k0 - c0, k1 - c0)
                                S.dve(lambda e: e.tensor_tensor(p1.t[:, :w_], pr_.t[:, :w_], wr, ALU.mult), [pr_, Wre], [p1])
                                S.dve(lambda e: e.tensor_tensor(p2.t[:, :w_], pi_.t[:, :w_], wi, ALU.mult), [pi_, Wim], [p2])
                                S.dve(lambda e: e.tensor_tensor(p3.t[:, :w_], pi_.t[:, :w_], wr, ALU.mult), [pi_, Wre], [p3])
                                S.dve(lambda e: e.tensor_tensor(p4.t[:, :w_], pr_.t[:, :w_], wi, ALU.mult), [pr_, Wim], [p4])
                                S.op("pool", lambda e: e.tensor_tensor(Zre.t[:, zo], p1.t[:, :w_], p2.t[:, :w_], ALU.subtract), [p1, p2], [Zre])
                                S.op("pool", lambda e: e.tensor_tensor(Zim.t[:, zo], p3.t[:, :w_], p4.t[:, :w_], ALU.add), [p3, p4], [Zim])
                            rb = rr_.t[:, d_, st:st + 1].to_broadcast([128, Ls])
                            for (Zp, Gp, ci_) in ((Zre, Gre, 0), (Zim, Gim, 1)):
                                init = 0.0 if prev_carry is None else prev_carry.t[:, ci_:ci_ + 1]
                                rd = [Zp, rr_] + ([] if prev_carry is None else [prev_carry])
                                if not rev:
                                    S.dve(lambda e: e.tensor_tensor_scan(Gp.t[:, 0:Ls], rb, Zp.t[:, 0:Ls], init, ALU.mult, ALU.add), rd, [Gp])
                                else:
                                    S.dve(lambda e: e.tensor_tensor_scan(Gp.t[:, 0:Ls][:, ::-1], rb, Zp.t[:, 0:Ls][:, ::-1], init, ALU.mult, ALU.add), rd, [Gp])
                            e0 = 0 if rev else Ls - 1
                            cy = carry[cb % 2]
                            cb += 1
                            mr, mi = Mre.t[:, Ls:Ls + 1], Mim.t[:, Ls:Ls + 1]
                            S.dve(lambda e: e.tensor_scalar_mul(ctmp.t[:, 0:1], Gim.t[:, e0:e0 + 1], mi), [Gim, Mim], [ctmp])
                            S.dve(lambda e: e.scalar_tensor_tensor(cy.t[:, 0:1], Gre.t[:, e0:e0 + 1], mr, ctmp.t[:, 0:1], ALU.mult, ALU.subtract), [Gre, Mre, ctmp], [cy])
                            S.dve(lambda e: e.tensor_scalar_mul(ctmp.t[:, 1:2], Gre.t[:, e0:e0 + 1], mi), [Gre, Mim], [ctmp])
                            S.dve(lambda e: e.scalar_tensor_tensor(cy.t[:, 1:2], Gim.t[:, e0:e0 + 1], mr, ctmp.t[:, 1:2], ALU.mult, ALU.add), [Gim, Mre, ctmp], [cy])
                            prev_carry = cy
                            if not rev:
                                mre_, mim_ = Mre.t[:, 0:Ls], Mim.t[:, 0:Ls]
                            else:
                                mre_, mim_ = Mre.t[:, 0:Ls][:, ::-1], Mim.t[:, 0:Ls][:, ::-1]
                            S.dve(lambda e: e.tensor_tensor(Q1.t[:, :Ls], Gre.t[:, :Ls], mre_, ALU.mult), [Gre, Mre], [Q1])
                            S.dve(lambda e: e.tensor_tensor(Q2.t[:, :Ls], Gim.t[:, :Ls], mim_, ALU.mult), [Gim, Mim], [Q2])
                            S.dve(lambda e: e.tensor_tensor(Q3.t[:, :Ls], Gre.t[:, :Ls], mim_, ALU.mult), [Gre, Mim], [Q3])
                            S.dve(lambda e: e.tensor_tensor(Q4.t[:, :Ls], Gim.t[:, :Ls], mre_, ALU.mult), [Gim, Mre], [Q4])
                            for k0 in range(0, Ls, 512):
                                w_ = min(512, Ls - k0)
                                yp = self.ps[4 + yi % 2]
                                yi += 1
                                for qi, (Qx, lw) in enumerate(((Q1, CT.t[:, 0, st, :]), (Q2, nCre.t[:, st, :]), (Q3, CT.t[:, 1, st, :]), (Q4, CT.t[:, 1, st, :]))):
                                    S.mm(yp.t[0:32, :w_], lw, Qx.t[:, k0:k0 + w_], start=(qi == 0), stop=(qi == 3), reads=[CT, nCre, Qx], writes=[yp], inc=(qi == 3))
                                ys = yacc.t[:, c0 + k0:c0 + k0 + w_]
                                if d_ == 0:
                                    S.act(ys, yp.t[0:32, :w_], AF.Copy, [yp], [yacc])
                                else:
                                    S.dve(lambda e: e.tensor_tensor(ys, ys, yp.t[0:32, :w_], ALU.add), [yacc, yp], [yacc])
                    S.dma("act", self.yT[st // 4, 32 * (st % 4):32 * (st % 4) + 32, :], yacc.t[:], reads=[yacc], writes=[self.dr("yT", t[0]) for t in TILES])

                tables(0, 0)
                pipelined(128, load, comp, 1)
        S.barrier()
        with contextlib.ExitStack() as lsC:
            dsk = S.sb("s5d", [128, KC], F32, lsC)
            S.dma("sp", dsk.t[:], s5["d"], writes=[dsk])
            mv = self.modv[l]
            for tile in self.tiles:
                c0, n, r = tile
                with contextlib.ExitStack() as ls0:
                    gT = S.sb("gT", [128, KC, 512], BF16, ls0)
                    with contextlib.ExitStack() as ls:
                        xb = [S.sb("fx", [128, 512], F32, ls) for _ in range(3)]
                        yb = [S.sb("fy", [128, 512], F32, ls) for _ in range(3)]
                        hb = [S.sb("fh", [128, 512], F32, ls) for _ in range(2)]

                        def loadf(j):
                            S.dma("sp", xb[j % 3].t[:, :n], self.cur[j, :, c0:c0 + n], reads=[self.dr(self.cur_name, c0)], writes=[xb[j % 3]])
                            S.dma("sp", yb[j % 3].t[:, :n], self.yT[j, :, c0:c0 + n], reads=[self.dr("yT", c0)], writes=[yb[j % 3]])

                        def compf(j):
                            x_, y_, h_ = xb[j % 3], yb[j % 3], hb[j % 2]
                            S.act(h_.t[:, :n], x_.t[:, :n], AF.Identity, [x_, mv], [h_], scale=mv.t[:, 1, j, r:r + 1], bias=mv.t[:, 0, j, r:r + 1])
                            S.dve(lambda e: e.scalar_tensor_tensor(h_.t[:, :n], h_.t[:, :n], dsk.t[:, j:j + 1], y_.t[:, :n], ALU.mult, ALU.add), [h_, dsk, y_], [h_])
                            S.act(gT.t[:, j, :n], h_.t[:, :n], AF.Gelu, [h_], [gT])

                        pipelined(KC, loadf, compf, 2)
                    S.barrier()
                    with contextlib.ExitStack() as ls:
                        wb = [S.sb("wgl", [128, 2 * KC * 128], BF16, ls) for _ in range(2)]
                        sgb = [S.sb("sgb", [128, 512], F32, ls) for _ in range(2)]
                        yj = [S.sb("yj", [128, 512], F32, ls) for _ in range(2)]
                        bufs = [[S.sb("e%d" % q, [128, 512], F32, ls) for q in range(4)] for _ in range(2)]
                        stats = [self.ps[6], self.ps[7]]
                        cnt = [0]

                        fresh = {}

                        def loadg(b):
                            fresh[b] = self.wload(s5["w_glu"], b, wb[b % 2], 2 * KC * 128)

                        def compg(b):
                            wt = wb[b % 2]
                            if fresh[b]:
                                self.wstore(s5["w_glu"], b, wt, 2 * KC * 128)
                            pa = self.ps[(cnt[0] % 2) * 2]
                            pg = self.ps[(cnt[0] % 2) * 2 + 1]
                            s_, y_ = sgb[cnt[0] % 2], yj[cnt[0] % 2]
                            cnt[0] += 1
                            for half, ps in ((0, pa), (1, pg)):
                                for kc in range(KC):
                                    o = (half * KC + kc) * 128
                                    S.mm(ps.t[:, :n], wt.t[:, o:o + 128], gT.t[:, kc, :n], start=(kc == 0), stop=(kc == KC - 1),
                                         reads=[wt, gT], writes=[ps], inc=(kc == KC - 1))
                            S.act(s_.t[:, :n], pg.t[:, :n], AF.Sigmoid, [pg], [s_])
                            S.dve(lambda e: e.tensor_tensor(y_.t[:, :n], pa.t[:, :n], s_.t[:, :n], ALU.mult), [pa, s_], [y_])
                            self.resid_epilogue(l, 0, tile, b, y_.t[:, :n], y_, self.cur, self.cur_name, bufs[b % 2], stats, b == 0, b == KC - 1)

                        pipelined(KC, loadg, compg, 1)
                        self.ln_finish_stats(n, stats)
                S.barrier()
                self.ffn_tail(l, tile)

    def adaln(self):
        S = self.S
        with contextlib.ExitStack() as ls:
            cv = S.sb("cv", [128, KC, 2], F32, ls)
            scv = S.sb("scv", [128, KC, 2], F32, ls)
            S.dma("sp", cv.t[:], self.cvT, writes=[cv])
            S.act(scv.t[:], cv.t[:], AF.Silu, [cv], [scv])
            dn = S.sb("dn", [128, KC, 256], F32, ls)
            ups = [S.sb("up", [128, 2, D], F32, ls) for _ in range(2)]
            bT = S.sb("bT", [128, 192], F32, ls)
            tT = S.sb("tT", [128, 2, 2], F32, ls)
            pi = 0
            for l in self.layers:
                W = self.W[l]
                S.dma("sp", dn.t[:], W["down"], writes=[dn])
                S.dma("sp", bT.t[:], W["b"], writes=[bT])
                ps = self.ps[0]
                for rc in range(2):
                    for kc in range(KC):
                        S.mm(ps.t[:, rc * 2:rc * 2 + 2], dn.t[:, kc, rc * 128:(rc + 1) * 128], scv.t[:, kc, :],
                             start=(kc == 0), stop=(kc == KC - 1), reads=[dn, scv], writes=[ps], inc=(kc == KC - 1))
                S.dve(lambda e: e.tensor_copy(tT.t[:].rearrange("p a b -> p (a b)"), ps.t[:, 0:4]), [ps], [tT])
                mv = self.modv[l]
                for j in range(6):
                    up = ups[pi % 2]
                    S.dma("sp", up.t[:], W["up"][:, :, j * D:(j + 1) * D], writes=[up])
                    mp = self.ps[1 + pi % 2]
                    pi += 1
                    for n in range(KC):
                        for rc in range(2):
                            S.mm(mp.t[:, n * 2:n * 2 + 2], up.t[:, rc, n * 128:(n + 1) * 128], tT.t[:, rc, :],
                                 start=(rc == 0), stop=(rc == 1), reads=[up, tT], writes=[mp], inc=(rc == 1 and n == KC - 1))
                    for r in range(2):
                        S.dve(lambda e: e.tensor_tensor(mv.t[:, j, :, r], mp.t[:, 0:2 * KC].rearrange("p (n r) -> p n r", r=2)[:, :, r],
                                                        bT.t[:, j * KC:(j + 1) * KC], ALU.add), [mp, bT], [mv])
                    if j in (1, 4):
                        S.dve(lambda e: e.tensor_scalar_add(mv.t[:, j], mv.t[:, j], 1.0), [mv], [mv])
        S.barrier()

    def mod_to_hT(self, l, which, src, src_name, tile, hT, ls, extra=None):
        S = self.S
        c0, n, r = tile
        mv = self.modv[l]
        jsh, jsc = (0, 1) if which == 0 else (3, 4)
        bufs = [S.sb("xl", [128, 8, 512], F32, ls) for _ in range(2)]
        for g in range(4):
            b = bufs[g % 2]
            S.dma("sp", b.t[:, :, :n], src[g * 8:(g + 1) * 8, :, c0:c0 + n].rearrange("k p t -> p k t"),
                  reads=[self.dr(src_name, c0)], writes=[b])
            for k in range(8):
                kc = g * 8 + k
                S.act(hT.t[:, kc, :n], b.t[:, k, :n], AF.Identity, [b, mv], [hT],
                      scale=mv.t[:, jsc, kc, r:r + 1], bias=mv.t[:, jsh, kc, r:r + 1])

    def resid_epilogue(self, l, which, tile, j, y_ap, y_res, src, src_name, bufs, stats, first, last):
        S = self.S
        c0, n, r = tile
        mv = self.modv[l]
        jg = 2 if which == 0 else 5
        xc, tmp, rt, rsq = bufs
        S.dma("sp", xc.t[:, :n], src[j, :, c0:c0 + n], reads=[self.dr(src_name, c0)], writes=[xc])
        S.act(tmp.t[:, :n], y_ap, AF.Identity, [y_res, mv], [tmp], scale=mv.t[:, jg, j, r:r + 1])
        S.dve(lambda e: e.scalar_tensor_tensor(rt.t[:, :n], xc.t[:, :n], ALPHA, tmp.t[:, :n], ALU.mult, ALU.add), [xc, tmp], [rt])
        S.act(rsq.t[:, :n], rt.t[:, :n], AF.Square, [rt], [rsq])
        S.mm(stats[0].t[:, :n], self.ones_f.t[:], rt.t[:, :n], start=first, stop=last, reads=[rt, self.ones_f], writes=[stats[0]], inc=False)
        S.mm(stats[1].t[:, :n], self.ones_f.t[:], rsq.t[:, :n], start=first, stop=last, reads=[rsq, self.ones_f], writes=[stats[1]], inc=True)
        S.dma("act", self.rT[j, :, c0:c0 + n], rt.t[:, :n], reads=[rt], writes=[self.dr("rT", c0)])

    def ln_finish_stats(self, n, stats):
        S = self.S
        m, rs = self.mean_t, self.rstd_t
        S.dve(lambda e: e.tensor_scalar_mul(m.t[:, :n], stats[0].t[:, :n], 1.0 / D), [stats[0]], [m])
        S.dve(lambda e: e.tensor_tensor(rs.t[:, :n], m.t[:, :n], m.t[:, :n], ALU.mult), [m], [rs])
        S.dve(lambda e: e.scalar_tensor_tensor(rs.t[:, :n], stats[1].t[:, :n], 1.0 / D, rs.t[:, :n], ALU.mult, ALU.subtract), [stats[1], rs], [rs])
        S.act(rs.t[:, :n], rs.t[:, :n], AF.Sqrt, [rs], [rs], bias=self.eps_t.t[:, 0:1])
        S.dve(lambda e: e.reciprocal(rs.t[:, :n], rs.t[:, :n]), [rs], [rs])

    def ln_apply(self, l, which, tile, dst, dst_name, ls, hT=None, lnext=None):
        S = self.S
        c0, n, r = tile
        m, rs = self.mean_t, self.rstd_t
        gi = (l * 2 + which) * KC
        rb = [S.sb("rl", [128, 512], F32, ls) for _ in range(3)]
        t2 = [S.sb("t2", [128, 512], F32, ls) for _ in range(2)]
        xn = [S.sb("xn", [128, 512], F32, ls) for _ in range(2)]
        mv = self.modv[l]

        def load(j):
            S.dma("sp", rb[j % 3].t[:, :n], self.rT[j, :, c0:c0 + n], reads=[self.dr("rT", c0)], writes=[rb[j % 3]])

        def comp(j):
            a, b, c = rb[j % 3], t2[j % 2], xn[j % 2]
            S.dve(lambda e: e.tensor_tensor(b.t[:, :n], a.t[:, :n], m.t[:, :n], ALU.subtract), [a, m], [b])
            S.dve(lambda e: e.tensor_tensor(b.t[:, :n], b.t[:, :n], rs.t[:, :n], ALU.mult), [b, rs], [b])
            S.act(c.t[:, :n], b.t[:, :n], AF.Identity, [b, self.lng, self.lnb], [c],
                  scale=self.lng.t[:, gi + j:gi + j + 1], bias=self.lnb.t[:, gi + j:gi + j + 1])
            S.dma("act", dst[j, :, c0:c0 + n], c.t[:, :n], reads=[c], writes=[self.dr(dst_name, c0)])
            if hT is not None:
                S.act(hT.t[:, j, :n], c.t[:, :n], AF.Identity, [c, mv], [hT],
                      scale=mv.t[:, 4, j, r:r + 1], bias=mv.t[:, 3, j, r:r + 1])

        pipelined(KC, load, comp, 2)

    def wload(self, w_ap, b, buf, width):
        S = self.S
        key = w_ap.tensor.name
        if not hasattr(self, "wc"):
            self.wc, self.wc_done = {}, {}
        if key not in self.wc:
            self.wc[key] = self.dtmp("wc_" + key, [w_ap.shape[0], 128, width], BF16)
            self.wc_done[key] = set()
        if b in self.wc_done[key]:
            S.dma("pool", buf.t[:, :width], self.wc[key][b], reads=[self.dr("wc_" + key, b)], writes=[buf])
            return False
        S.dma("pool", buf.t[:, :width], w_ap[b], writes=[buf])
        return True

    def wstore(self, w_ap, b, buf, width):
        key = w_ap.tensor.name
        self.S.dma("act", self.wc[key][b], buf.t[:, :width], reads=[buf], writes=[self.dr("wc_" + key, b)])
        self.wc_done[key].add(b)

    def gemm_fm(self, w_ap, nblk, kcn, nw, hT, n, wbufs, psums, epi):
        S = self.S
        nb = len(wbufs)
        sub_n = nw // 128
        cnt = [0]

        fresh = {}

        def load(b):
            fresh[b] = self.wload(w_ap, b, wbufs[b % nb], kcn * nw)

        def comp(b):
            wt = wbufs[b % nb]
            if fresh[b]:
                self.wstore(w_ap, b, wt, kcn * nw)
            for sub in range(sub_n):
                ps = psums[cnt[0] % len(psums)]
                cnt[0] += 1
                for kc in range(kcn):
                    S.mm(ps.t[:, :n], wt.t[:, kc * nw + sub * 128:kc * nw + (sub + 1) * 128], hT.t[:, kc, :n],
                         start=(kc == 0), stop=(kc == kcn - 1), reads=[wt, hT], writes=[ps], inc=(kc == kcn - 1))
                epi(b * sub_n + sub, ps)

        pipelined(nblk, load, comp, nb - 1)

    def ffn_tail(self, l, tile):
        S = self.S
        c0, n, r = tile
        W = self.W[l]
        with contextlib.ExitStack() as ls0:
            hid = S.sb("hid", [128, FC, 512], BF16, ls0)
            with contextlib.ExitStack() as ls:
                hT = S.sb("hT", [128, KC, 512], BF16, ls)
                self.ln_apply(l, 0, tile, self.mid, self.mid_name, ls, hT=hT)
                wb = [S.sb("wfi", [128, 2 * KC * 128], BF16, ls) for _ in range(2)]
                sg = [S.sb("sg", [128, 512], F32, ls) for _ in range(2)]
                cnt = [0]

                fresh = {}

                def load(b):
                    fresh[b] = self.wload(W["w_in"], b, wb[b % 2], 2 * KC * 128)

                def comp(b):
                    wt = wb[b % 2]
                    if fresh[b]:
                        self.wstore(W["w_in"], b, wt, 2 * KC * 128)
                    pg = self.ps[(cnt[0] % 2) * 2]
                    pu = self.ps[(cnt[0] % 2) * 2 + 1]
                    s = sg[cnt[0] % 2]
                    cnt[0] += 1
                    for half, ps in ((0, pg), (1, pu)):
                        for kc in range(KC):
                            o = (half * KC + kc) * 128
                            S.mm(ps.t[:, :n], wt.t[:, o:o + 128], hT.t[:, kc, :n], start=(kc == 0), stop=(kc == KC - 1),
                                 reads=[wt, hT], writes=[ps], inc=(kc == KC - 1))
                    S.act(s.t[:, :n], pg.t[:, :n], AF.Silu, [pg], [s])
                    S.dve(lambda e: e.tensor_tensor(hid.t[:, b, :n], s.t[:, :n], pu.t[:, :n], ALU.mult), [s, pu], [hid])

                pipelined(FC, load, comp, 1)
            S.barrier()
            with contextlib.ExitStack() as ls:
                wb = [S.sb("wfo", [128, FC * 128], BF16, ls) for _ in range(2)]
                bufs = [[S.sb("e%d" % k, [128, 512], F32, ls) for k in range(4)] for _ in range(2)]
                stats = [self.ps[6], self.ps[7]]

                def epi(c, ps):
                    self.resid_epilogue(l, 1, tile, c, ps.t[:, :n], ps, self.mid, self.mid_name, bufs[c % 2], stats, c == 0, c == KC - 1)

                self.gemm_fm(W["w_out"], KC, FC, 128, hid, n, wb, self.ps[0:4], epi)
                self.ln_finish_stats(n, stats)
        S.barrier()
        with contextlib.ExitStack() as ls:
            self.ln_apply(l, 1, tile, self.nxt, self.nxt_name, ls)
        S.barrier()

    def layer2(self, l):
        S = self.S
        cm = self.cm
        with contextlib.ExitStack() as lsC:
            lngb = S.sb("cmlngb", [128, 2 * KC], F32, lsC)
            wsT = S.sb("cmwsT", [128, 16 * 128], BF16, lsC)
            Cj = S.sb("cmCj", [128, KC, 128], F32, lsC)
            S.dma("sp", lngb.t[:], cm["lngb"], writes=[lngb])
            S.dma("pool", wsT.t[:], cm["wsT"], writes=[wsT])
            with contextlib.ExitStack() as ls:
                bsb = S.sb("bsb", [128, 16 * 128], F32, ls)
                rsum = S.sb("rsum", [128, 16 * 128], F32, ls)
                S.dma("sp", bsb.t[:], cm["bs"].partition_broadcast(128), writes=[bsb])
                for q in range(4):
                    ps = self.ps[q]
                    S.mm(ps.t[:, :], self.ones_b.t[:], wsT.t[:, q * 512:(q + 1) * 512], start=True, stop=True,
                         reads=[wsT, self.ones_b], writes=[ps])
                    S.dve(lambda e: e.tensor_copy(rsum.t[:, q * 512:(q + 1) * 512], ps.t[:, :]), [ps], [rsum])
                for j in range(KC):
                    g = j // 2
                    S.dve(lambda e: e.scalar_tensor_tensor(Cj.t[:, j, :], rsum.t[:, g * 128:(g + 1) * 128], lngb.t[:, KC + j:KC + j + 1],
                                                           bsb.t[:, g * 128:(g + 1) * 128], ALU.mult, ALU.add), [rsum, lngb, bsb], [Cj])
            S.barrier()
            for tile in self.tiles:
                c0, n, r = tile
                nch = n // 128
                with contextlib.ExitStack() as ls0:
                    hT = S.sb("hT", [128, KC, 512], BF16, ls0)
                    vn = S.sb("vn", [128, 4, D], BF16, ls0)
                    with contextlib.ExitStack() as ls:
                        self.mod_to_hT(l, 0, self.cur, self.cur_name, tile, hT, ls)
                    S.barrier()
                    with contextlib.ExitStack() as ls:
                        v = S.sb("v", [128, 4, D], F32, ls)
                        wb = [S.sb("wv", [128, KC * 256], BF16, ls) for _ in range(2)]
                        cnt = [0]

                        fresh = {}

                        def load(b):
                            fresh[b] = self.wload(cm["w_v"], b, wb[b % 2], KC * 256)

                        def comp(b):
                            wt = wb[b % 2]
                            if fresh[b]:
                                self.wstore(cm["w_v"], b, wt, KC * 256)
                            for ch in range(nch):
                                ps = self.ps[cnt[0] % 8]
                                cnt[0] += 1
                                for kc in range(KC):
                                    S.mm(ps.t[:, :256], hT.t[:, kc, ch * 128:(ch + 1) * 128], wt.t[:, kc * 256:(kc + 1) * 256],
                                         start=(kc == 0), stop=(kc == KC - 1), reads=[wt, hT], writes=[ps], inc=(kc == KC - 1))
                                S.act(v.t[:, ch, b * 256:(b + 1) * 256], ps.t[:, :256], AF.Gelu, [ps], [v])

                        pipelined(D // 256, load, comp, 1)
                        st6 = S.sb("st6", [128, 8, 6], F32, ls)
                        mvv = S.sb("mvv", [128, 4, 2], F32, ls)
                        nb_ = S.sb("nb_", [128, 4, 2], F32, ls)
                        for ch in range(nch):
                            for q in range(8):
                                S.dve(lambda e: e.bn_stats(st6.t[:, q, :], v.t[:, ch, q * 512:(q + 1) * 512]), [v], [st6])
                            S.dve(lambda e: e.bn_aggr(mvv.t[:, ch, :], st6.t[:].rearrange("p a b -> p (a b)")), [st6], [mvv])
                            S.act(nb_.t[:, ch, 0:1], mvv.t[:, ch, 1:2], AF.Sqrt, [mvv], [nb_], bias=self.eps_t.t[:, 0:1])
                            S.dve(lambda e: e.reciprocal(nb_.t[:, ch, 0:1], nb_.t[:, ch, 0:1]), [nb_], [nb_])
                            S.dve(lambda e: e.scalar_tensor_tensor(nb_.t[:, ch, 1:2], mvv.t[:, ch, 0:1], -1.0, nb_.t[:, ch, 0:1], ALU.mult, ALU.mult), [mvv, nb_], [nb_])
                            S.act(vn.t[:, ch, :], v.t[:, ch, :], AF.Identity, [v, nb_], [vn], scale=nb_.t[:, ch, 0:1], bias=nb_.t[:, ch, 1:2])
                    S.barrier()
                    with contextlib.ExitStack() as ls:
                        vmT = S.sb("vmT", [128, KC, 512], BF16, ls)
                        k = 0
                        for ch in range(nch):
                            for j in range(KC):
                                g = j // 2
                                ps = self.ps[k % 8]
                                k += 1
                                S.mm(ps.t[:, :128], vn.t[:, ch, j * 128:(j + 1) * 128], wsT.t[:, g * 128:(g + 1) * 128], start=True, stop=True,
                                     reads=[vn, wsT], writes=[ps])
                                S.dve(lambda e: e.scalar_tensor_tensor(vmT.t[:, j, ch * 128:(ch + 1) * 128], ps.t[:, :128], lngb.t[:, j:j + 1],
                                                                       Cj.t[:, j, :], ALU.mult, ALU.add), [ps, lngb, Cj], [vmT])
                        wb = [S.sb("wu", [128, KC * 256], BF16, ls) for _ in range(3)]
                        ug = [S.sb("ug", [128, 512], F32, ls) for _ in range(2)]

                        def epi_u(c, ps):
                            u_ = ug[c % 2]
                            S.act(u_.t[:, :n], ps.t[:, :n], AF.Gelu, [ps], [u_])
                            S.dve(lambda e: e.tensor_tensor(vmT.t[:, c, :n], u_.t[:, :n], vmT.t[:, c, :n], ALU.mult), [u_, vmT], [vmT])

                        self.gemm_fm(cm["w_u"], KC // 2, KC, 256, hT, n, wb, self.ps[0:4], epi_u)
                        bufs = [[S.sb("e%d" % q, [128, 512], F32, ls) for q in range(4)] for _ in range(2)]
                        stats = [self.ps[6], self.ps[7]]

                        def epi_o(c, ps):
                            self.resid_epilogue(l, 0, tile, c, ps.t[:, :n], ps, self.cur, self.cur_name, bufs[c % 2], stats, c == 0, c == KC - 1)

                        self.gemm_fm(cm["w_out"], KC // 2, KC, 256, vmT, n, wb, self.ps[0:4], epi_o)
                        self.ln_finish_stats(n, stats)
                S.barrier()
                self.ffn_tail(l, tile)


def tile_fm(W, nw):
    K, N = W.shape
    return np.ascontiguousarray(W.reshape(K // 128, 128, N // nw, nw).transpose(2, 1, 0, 3)).reshape(N // nw, 128, (K // 128) * nw)


def tile_pair(Wa, Wb):
    K, N = Wa.shape
    a = Wa.reshape(K // 128, 128, N // 128, 128).transpose(2, 1, 0, 3)
    b = Wb.reshape(K // 128, 128, N // 128, 128).transpose(2, 1, 0, 3)
    return np.ascontiguousarray(np.stack([a, b], axis=2)).reshape(N // 128, 128, 2 * (K // 128) * 128)


def fm_vec(v):
    sh = v.shape[:-1]
    return np.ascontiguousarray(np.moveaxis(v.reshape(sh + (KC, 128)), -1, 0))


def common_inputs(inp, layers):
    d = {}
    d["lngT"] = fm_vec(inp["ln_g"]).reshape(128, -1)
    d["lnbT"] = fm_vec(inp["ln_b"]).reshape(128, -1)
    for l in layers:
        d["ada_down%d" % l] = np.ascontiguousarray(inp["ada_down"][l].reshape(KC, 128, 256).transpose(1, 0, 2))
        d["ada_up%d" % l] = np.ascontiguousarray(inp["ada_up"][l].reshape(2, 128, 6 * D).transpose(1, 0, 2))
        d["ada_b%d" % l] = np.ascontiguousarray(inp["ada_b"][l].reshape(192, 128).T)
        wi = inp["ffn_w_in"][l]
        d["ffn_in%d" % l] = tile_pair(wi[:, :F], wi[:, F:])
        d["ffn_out%d" % l] = tile_fm(inp["ffn_w_out"][l], 128)
    if 2 in layers:
        w = inp["cm_w_in"][0]
        d["cm_wu"] = tile_fm(w[:, :D], 256)
        d["cm_wv"] = tile_fm(w[:, D:], 256)
        d["cm_wout"] = tile_fm(inp["cm_w_out"][0], 256)
        d["cm_lngb"] = np.concatenate([fm_vec(inp["cm_ln_g"][0]), fm_vec(inp["cm_ln_b"][0])], axis=1)
        d["cm_wsT"] = np.ascontiguousarray(inp["cm_w_s"][0].transpose(2, 0, 1)).reshape(128, 16 * 128)
        d["cm_bs"] = np.ascontiguousarray(inp["cm_b_s"][0].reshape(1, 16 * 128))
    if 0 in layers:
        G2 = 128
        def per_state(a):
            return a.reshape(2, G2, 128).transpose(2, 0, 1)
        are, aim = inp["ssm_a_re"][0], inp["ssm_a_im"][0]
        ldt = np.broadcast_to(inp["ssm_log_dt"][0][:, :, None], (2, 256, 64))
        par = np.stack([per_state(are), per_state(aim), per_state(np.ascontiguousarray(ldt))], axis=2)
        d["s5_par"] = np.ascontiguousarray(par).reshape(128, 2 * 3 * 128).astype(np.float32)
        BT = np.zeros((128, 32, 2, 128), np.float32)
        for part, nm in enumerate(("ssm_b_re", "ssm_b_im")):
            Bp = inp[nm][0].reshape(128, 2, 64, 16)
            for g2 in range(2):
                BT[:, g2 * 16:(g2 + 1) * 16, part, g2 * 64:(g2 + 1) * 64] = Bp[:, g2].transpose(0, 2, 1)
        d["s5_BT"] = BT.reshape(128, 32, 2 * 128)
        CT = np.zeros((128, 2, 128, 32), np.float32)
        for part, nm in enumerate(("ssm_c_re", "ssm_c_im")):
            Cp = inp[nm][0].reshape(128, 2, 16, 64)
            for g2 in range(2):
                CT[g2 * 64:(g2 + 1) * 64, part, :, g2 * 16:(g2 + 1) * 16] = Cp[:, g2].transpose(2, 0, 1)
        d["s5_CT"] = CT.reshape(128, 2 * 128 * 32)
        d["s5_d"] = fm_vec(inp["ssm_d"][0])
        w = inp["ssm_w_glu"][0]
        d["s5_wglu"] = tile_pair(w[:, :D], w[:, D:])
    if 1 in layers:
        w = inp["da_w_qkv"][0]
        d["da_wqk"] = tile_fm(w[:, :2 * D], 256)
        d["da_wv"] = tile_fm(w[:, 2 * D:], 256)
        d["da_wo"] = tile_fm(inp["da_w_o"][0], 256)
        d["da_lam"] = np.ascontiguousarray(inp["da_lambda"][0].T)
        d["da_subln"] = np.ascontiguousarray(inp["da_subln_g"][0].reshape(2, 128).T)
        t = np.arange(L)
        row, col = (t // GRID).astype(np.float32), (t % GRID).astype(np.float32)
        inv = (np.float32(10000.0) ** (-np.arange(32, dtype=np.float32) * np.float32(2.0) / np.float32(64))).astype(np.float32)
        C = np.zeros((128, L), np.float32)
        Sn = np.zeros((128, L), np.float32)
        P = np.zeros((128, 128), np.float32)
        for dd in range(128):
            pos = row if dd < 64 else col
            f = dd % 32
            ang = (pos * inv[f]).astype(np.float32)
            C[dd] = np.cos(ang)
            first = (dd % 64) < 32
            Sn[dd] = -np.sin(ang) if first else np.sin(ang)
            P[dd + 32 if first else dd - 32, dd] = 1.0
        d["ropeC"], d["ropeS"], d["ropeP"] = C, Sn, P
    if 3 in layers:
        w = inp["na_w_qkv"][0]
        d["na_wqk"] = tile_fm(w[:, :2 * D], 256)
        d["na_wv"] = tile_fm(w[:, 2 * D:], 256)
        d["na_wo"] = tile_fm(inp["na_w_o"][0], 256)
        rpb = inp["na_rpb"][0]
        kc = np.arange(64)[:, None]
        qc = np.arange(64)[None, :]
        cs = np.clip(qc - 8, 0, GRID - 16)
        valid = (kc >= cs) & (kc <= cs + 15)
        dc = np.clip(kc - qc + 15, 0, 30)
        T = rpb[:, :, dc] * valid[None, None]
        T = np.concatenate([T, np.zeros((32, 1, 64, 64), np.float32)], axis=1)
        TT = np.stack([T[:, 0:15], T[:, 1:16]], axis=1)
        d["na_bias"] = np.ascontiguousarray(TT.transpose(0, 1, 3, 2, 4)).reshape(32, 128, 15 * 64).astype(np.float32)
        m = np.where(valid, 0.0, -30000.0).astype(np.float32)
        M = np.broadcast_to(m[None, :, None, :], (2, 64, 15, 64)).copy()
        M[1, :, 14, :] = -30000.0
        d["na_mask"] = np.ascontiguousarray(M).reshape(128, 15 * 64)
    return d


def core_inputs(inp, b, x_lat=None, x_ctx=None):
    xl = inp["x"][b] if x_lat is None else x_lat
    xc = inp["ctx"][b] if x_ctx is None else x_ctx
    xt = np.concatenate([xc, xl], axis=0)
    d = {"xT": np.ascontiguousarray(xt.T).reshape(KC, 128, NTOK)}
    cv = np.stack([inp["c"][b], inp["c_ctx"]], axis=0)
    d["cvT"] = np.ascontiguousarray(fm_vec(cv).transpose(0, 2, 1))
    return d


def kernel(**inputs):
    inp = {k: np.asarray(v) for k, v in inputs.items()}
    layers = [0, 1, 2, 3]
    prog = Prog(layers)
    nc = prog.build()
    com = common_inputs(inp, layers)
    in_maps = []
    for b in range(2):
        m = dict(com)
        m.update(core_inputs(inp, b))
        in_maps.append({k: m[k] for k in prog.in_names})
    res = run_bass_kernel_spmd(nc, in_maps, core_ids=[0, 1])
    out = np.empty((2, L, D), np.float32)
    for b in range(2):
        out[b] = res.results[b]["outT"].reshape(D, L).T
    return out
```

```python
import contextlib
import math
import numpy as np
import concourse.bass as bass
import concourse.mybir as mybir
from concourse.bass_utils import run_bass_kernel_spmd

F32 = mybir.dt.float32
BF16 = mybir.dt.bfloat16
AF = mybir.ActivationFunctionType
ALU = mybir.AluOpType

D = 4096
KC = 32
L = 4096
CTX = 256
NTOK = CTX + L
F = 11008
FC = 86
DEPTH = 4
GRID = 64
ALPHA = (2 * DEPTH) ** 0.25
EPS = 1e-5
NT = 512
TILES = [(0, CTX, 1)] + [(CTX + NT * i, NT, 0) for i in range(L // NT)]
LAT_TILES = TILES[1:]


class Res:
    __slots__ = ("name", "w", "rs", "t")

    def __init__(self, name, t=None):
        self.name = name
        self.w = None
        self.rs = []
        self.t = t


class Sched:
    def __init__(self, nc, stack, n_dma_sems=14):
        self.nc = nc
        self.stack = stack
        self.E = {"pe": nc.tensor, "act": nc.scalar, "dve": nc.vector, "pool": nc.gpsimd, "sp": nc.sync}
        self.csem = {}
        self.cnt = {}
        for e in ("pe", "act", "dve", "pool"):
            self.csem[e] = stack.enter_context(nc.semaphore("c_" + e))
            self.cnt[e] = 0
        self.seen = {e: {} for e in self.E}
        self.dsems = {}
        self.didx = {}
        self.dval = {}
        for q in ("sp", "act", "pool"):
            self.dsems[q] = [stack.enter_context(nc.semaphore("d_%s%d" % (q, i))) for i in range(n_dma_sems)]
            self.didx[q] = 0
        self.ninst = 0
        self.uid = 0

    def sb(self, name, shape, dtype, stack=None):
        self.uid += 1
        t = (stack or self.stack).enter_context(self.nc.sbuf_tensor("%s_%d" % (name, self.uid), shape, dtype))
        return Res(name, t)

    def ps(self, name, shape, dtype=F32):
        t = self.stack.enter_context(self.nc.psum_tensor(name, shape, dtype))
        return Res(name, t)

    def _wait(self, e, tickets):
        need = {}
        seen = self.seen[e]
        for t in tickets:
            if t is None:
                continue
            sem, val = t
            if seen.get(sem, 0) >= val:
                continue
            if need.get(sem, 0) < val:
                need[sem] = val
        eng = self.E[e]
        for sem, val in need.items():
            eng.wait_ge(sem, val)
            seen[sem] = val
            self.ninst += 1

    def _deps(self, e, reads, writes):
        own = self.csem.get(e)
        deps = []
        for r in reads:
            t = r.w
            if t is not None:
                if t[0] is own and e == "pe":
                    continue
                deps.append(t)
        for w in writes:
            t = w.w
            if t is not None and t[0] is not own:
                deps.append(t)
            for t in w.rs:
                if t[0] is not own:
                    deps.append(t)
        return deps

    def _commit(self, ticket, reads, writes):
        for r in reads:
            r.rs.append(ticket)
            if len(r.rs) > 48:
                best = {}
                for s, v in r.rs:
                    if best.get(s, 0) < v:
                        best[s] = v
                r.rs = list(best.items())
        for w in writes:
            w.w = ticket
            w.rs = []

    def op(self, e, fn, reads=(), writes=(), inc=True):
        self._wait(e, self._deps(e, reads, writes))
        ins = fn(self.E[e])
        self.ninst += 1
        if inc:
            self.cnt[e] += 1
            ins.then_inc(self.csem[e], 1)
            ticket = (self.csem[e], self.cnt[e])
        else:
            ticket = (self.csem[e], self.cnt[e] + 1)
        self._commit(ticket, reads, writes)
        return ticket

    def mm(self, out, lhsT, rhs, start, stop, reads, writes, inc=True):
        return self.op("pe", lambda e: e.matmul(out, lhsT, rhs, start=start, stop=stop), reads, writes, inc)

    def act(self, out, in_, func, reads, writes, **kw):
        return self.op("act", lambda e: e.activation(out, in_, func, **kw), reads, writes)

    def dve(self, fn, reads, writes):
        return self.op("dve", fn, reads, writes)

    def dma(self, q, out, in_, reads=(), writes=()):
        i = self.didx[q]
        self.didx[q] = (i + 1) % len(self.dsems[q])
        sem = self.dsems[q][i]
        prev = self.dval.get(sem, 0)
        deps = self._deps(q, reads, writes)
        if prev:
            deps.append((sem, prev))
        self._wait(q, deps)
        self.E[q].dma_start(out=out, in_=in_).then_inc(sem, 16)
        self.ninst += 1
        self.dval[sem] = prev + 16
        ticket = (sem, prev + 16)
        self._commit(ticket, reads, writes)
        return ticket

    def barrier(self):
        tickets = [(s, v) for s, v in self.dval.items()]
        tickets += [(self.csem[e], self.cnt[e]) for e in self.csem if self.cnt[e] > 0]
        for e in ("sp", "act", "dve", "pool", "pe"):
            self._wait(e, [t for t in tickets if not (e in self.csem and t[0] is self.csem[e])])


def pipelined(n, load, compute, depth):
    for j in range(n + depth):
        if j < n:
            load(j)
        if j >= depth:
            compute(j - depth)


class Prog:
    def __init__(self, layers, tiles=None, final_out=True):
        self.layers = list(layers)
        self.tiles = tiles if tiles is not None else [TILES[1], TILES[0]] + TILES[2:]
        self.nc = bass.Bass("TRN2", target_bir_lowering=False)
        self.in_names = []
        self._pstats = []
        self.dres = {}
        self.final_out = final_out

    def din(self, name, shape, dtype=F32):
        self.in_names.append(name)
        return self.nc.dram_tensor(name, list(shape), dtype, kind="ExternalInput").ap()

    def dtmp(self, name, shape, dtype=F32):
        return self.nc.dram_tensor(name, list(shape), dtype, kind="Internal").ap()

    def dr(self, name, key=0):
        k = (name, key)
        if k not in self.dres:
            self.dres[k] = Res("%s_%s" % (name, key))
        return self.dres[k]

    def drall(self, name):
        return [self.dr(name, t[0]) for t in TILES]

    def build(self):
        nc = self.nc
        with contextlib.ExitStack() as st:
            S = self.S = Sched(nc, st)
            self.x_in = self.din("xT", [KC, 128, NTOK])
            self.cvT = self.din("cvT", [128, KC, 2])
            self.lngT = self.din("lngT", [128, DEPTH * 2 * KC])
            self.lnbT = self.din("lnbT", [128, DEPTH * 2 * KC])
            self.W = {}
            for l in self.layers:
                self.W[l] = {
                    "down": self.din("ada_down%d" % l, [128, KC, 256]),
                    "up": self.din("ada_up%d" % l, [128, 2, 6 * D]),
                    "b": self.din("ada_b%d" % l, [128, 192]),
                    "w_in": self.din("ffn_in%d" % l, [FC, 128, 2 * KC * 128]),
                    "w_out": self.din("ffn_out%d" % l, [KC, 128, FC * 128]),
                }
            self.out = nc.dram_tensor("outT", [KC, 128, L], F32, kind="ExternalOutput").ap()
            self.xT = [self.dtmp("xTa", [KC, 128, NTOK]), self.dtmp("xTb", [KC, 128, NTOK])]
            self.rT = self.dtmp("rT", [KC, 128, NTOK])
            self.xmid = self.dtmp("xmid", [KC, 128, NTOK])
            self.ps = [S.ps("ps%d" % i, [128, 512]) for i in range(8)]
            self.modv = {l: S.sb("modv%d" % l, [128, 6, KC, 2], F32) for l in self.layers}
            self.lng = S.sb("lng", [128, DEPTH * 2 * KC], F32)
            self.lnb = S.sb("lnb", [128, DEPTH * 2 * KC], F32)
            self.ones_f = S.sb("ones_f", [128, 128], F32)
            self.ones_b = S.sb("ones_b", [128, 128], BF16)
            self.mean_t = S.sb("mean_t", [128, 512], F32)
            self.rstd_t = S.sb("rstd_t", [128, 512], F32)
            self.eps_t = S.sb("eps_t", [128, 1], F32)
            S.dve(lambda e: e.memset(self.eps_t.t[:], EPS), [], [self.eps_t])
            S.dve(lambda e: e.memset(self.ones_f.t[:], 1.0), [], [self.ones_f])
            S.dve(lambda e: e.memset(self.ones_b.t[:], 1.0), [], [self.ones_b])
            S.dma("sp", self.lng.t[:], self.lngT, writes=[self.lng])
            S.dma("sp", self.lnb.t[:], self.lnbT, writes=[self.lnb])
            self.declare_layer_inputs()
            self.declare_attn_inputs()
            self.declare_s5_inputs()
            self.adaln()
            cur = self.x_in
            cur_name = "x_in"
            for l in self.layers:
                nxt = self.xT[l % 2]
                nxt_name = "xT%d" % (l % 2)
                self.cur, self.cur_name, self.nxt, self.nxt_name = cur, cur_name, nxt, nxt_name
                self.mid, self.mid_name = self.xmid, "xmid"
                getattr(self, "layer%d" % l)(l)
                cur, cur_name = nxt, nxt_name
            with contextlib.ExitStack() as ls:
                bufs = [S.sb("ob", [128, 8, 512], F32, ls) for _ in range(2)]
                i = 0
                for kc0 in range(0, KC, 8):
                    for (c0, n, r) in LAT_TILES:
                        if (c0, n, r) not in self.tiles:
                            continue
                        b = bufs[i % 2]
                        i += 1
                        S.dma("sp", b.t[:, :, :n], cur[kc0:kc0 + 8, :, c0:c0 + n].rearrange("k p t -> p k t"),
                              reads=[self.dr(cur_name, c0)], writes=[b])
                        S.dma("act", self.out[kc0:kc0 + 8, :, c0 - CTX:c0 - CTX + n].rearrange("k p t -> p k t"), b.t[:, :, :n],
                              reads=[b])
            S.barrier()
            self.ninst = S.ninst
        return nc

    def declare_layer_inputs(self):
        if 2 in self.layers:
            self.cm = {
                "w_u": self.din("cm_wu", [KC // 2, 128, KC * 256]),
                "w_v": self.din("cm_wv", [D // 256, 128, KC * 256]),
                "w_out": self.din("cm_wout", [KC // 2, 128, KC * 256]),
                "lngb": self.din("cm_lngb", [128, 2 * KC]),
                "wsT": self.din("cm_wsT", [128, 16 * 128]),
                "bs": self.din("cm_bs", [1, 16 * 128]),
            }

    def declare_attn_inputs(self):
        if 1 in self.layers or 3 in self.layers:
            self.qkT = self.dtmp("qkT", [64, 128, NTOK], BF16)
            self.Vd = self.dtmp("Vd", [NTOK, D], BF16)
            self.oT = self.dtmp("oT", [KC, 128, NTOK], BF16)
        if 1 in self.layers:
            self.da = {
                "w_qk": self.din("da_wqk", [32, 128, KC * 256]),
                "w_v": self.din("da_wv", [16, 128, KC * 256]),
                "w_o": self.din("da_wo", [16, 128, KC * 256]),
                "lam": self.din("da_lam", [128, 4]),
                "subln": self.din("da_subln", [128, 2]),
                "ropeC": self.din("ropeC", [128, L]),
                "ropeS": self.din("ropeS", [128, L]),
                "perm": self.din("ropeP", [128, 128]),
            }
        if 3 in self.layers:
            self.na = {
                "w_qk": self.din("na_wqk", [32, 128, KC * 256]),
                "w_v": self.din("na_wv", [16, 128, KC * 256]),
                "w_o": self.din("na_wo", [16, 128, KC * 256]),
                "bias": self.din("na_bias", [32, 128, 15 * 64]),
                "mask": self.din("na_mask", [128, 15 * 64]),
            }

    def qkv_project(self, l, w, rope):
        S = self.S
        with contextlib.ExitStack() as lsC:
            if rope:
                perm = S.sb("perm", [128, 128], F32, lsC)
                S.dma("sp", perm.t[:], self.da["perm"], writes=[perm])
            for tile in self.tiles_all:
                c0, n, r = tile
                nch = n // 128
                with contextlib.ExitStack() as ls0:
                    hT = S.sb("hT", [128, KC, 512], BF16, ls0)
                    with contextlib.ExitStack() as ls:
                        self.mod_to_hT(l, 0, self.cur, self.cur_name, tile, hT, ls)
                    S.barrier()
                    with contextlib.ExitStack() as ls:
                        wb = [S.sb("wqk", [128, KC * 256], BF16, ls) for _ in range(3)]
                        qo = [S.sb("qo", [128, 512], BF16, ls) for _ in range(3)]
                        do_rope = rope and r == 0
                        if do_rope:
                            cs = S.sb("ropec", [128, 512], F32, ls)
                            sn = S.sb("ropes", [128, 512], F32, ls)
                            S.dma("sp", cs.t[:, :n], self.da["ropeC"][:, c0 - CTX:c0 - CTX + n], writes=[cs])
                            S.dma("sp", sn.t[:, :n], self.da["ropeS"][:, c0 - CTX:c0 - CTX + n], writes=[sn])
                            q32 = [S.sb("q32", [128, 512], F32, ls) for _ in range(2)]
                            t1 = [S.sb("t1", [128, 512], F32, ls) for _ in range(2)]
                            t2 = [S.sb("t2r", [128, 512], F32, ls) for _ in range(2)]

                        def epi(c, ps):
                            o = qo[c % 3]
                            if do_rope:
                                a, b1, b2 = q32[c % 2], t1[c % 2], t2[c % 2]
                                rp = self.ps[4 + c % 2]
                                S.act(a.t[:, :n], ps.t[:, :n], AF.Copy, [ps], [a])
                                S.mm(rp.t[:, :n], perm.t[:], a.t[:, :n], start=True, stop=True, reads=[perm, a], writes=[rp])
                                S.dve(lambda e: e.tensor_tensor(b1.t[:, :n], a.t[:, :n], cs.t[:, :n], ALU.mult), [a, cs], [b1])
                                S.dve(lambda e: e.tensor_tensor(b2.t[:, :n], rp.t[:, :n], sn.t[:, :n], ALU.mult), [rp, sn], [b2])
                                S.dve(lambda e: e.tensor_tensor(o.t[:, :n], b1.t[:, :n], b2.t[:, :n], ALU.add), [b1, b2], [o])
                            else:
                                S.act(o.t[:, :n], ps.t[:, :n], AF.Copy, [ps], [o])
                            S.dma("act", self.qkT[c, :, c0:c0 + n], o.t[:, :n], reads=[o], writes=[self.dr("qkT", c0)])

                        self.gemm_fm(w["w_qk"], 32, KC, 256, hT, n, wb, self.ps[0:4], epi)
                        vo = [S.sb("vo", [128, 256], BF16, ls) for _ in range(3)]
                        cnt = [0]

                        fresh = {}

                        def load(b):
                            fresh[b] = self.wload(w["w_v"], b, wb[b % 3], KC * 256)

                        def comp(b):
                            wt = wb[b % 3]
                            if fresh[b]:
                                self.wstore(w["w_v"], b, wt, KC * 256)
                            for ch in range(nch):
                                ps = self.ps[cnt[0] % 4]
                                o = vo[cnt[0] % 3]
                                cnt[0] += 1
                                for kc in range(KC):
                                    S.mm(ps.t[:, :256], hT.t[:, kc, ch * 128:(ch + 1) * 128], wt.t[:, kc * 256:(kc + 1) * 256],
                                         start=(kc == 0), stop=(kc == KC - 1), reads=[wt, hT], writes=[ps], inc=(kc == KC - 1))
                                S.act(o.t[:], ps.t[:, :256], AF.Copy, [ps], [o])
                                S.dma("act", self.Vd[c0 + ch * 128:c0 + (ch + 1) * 128, b * 256:(b + 1) * 256], o.t[:], reads=[o],
                                      writes=[self.dr("Vd", c0)])

                        pipelined(16, load, comp, 2)
                S.barrier()

    def attn_out_tail(self, l, w, tiles):
        S = self.S
        for tile in tiles:
            c0, n, r = tile
            with contextlib.ExitStack() as ls:
                oin = S.sb("oin", [128, KC, 512], BF16, ls)
                S.dma("sp", oin.t[:, :, :n], self.oT[:, :, c0:c0 + n].rearrange("k p t -> p k t"), reads=[self.dr("oT", c0)], writes=[oin])
                wb = [S.sb("wo", [128, KC * 256], BF16, ls) for _ in range(3)]
                bufs = [[S.sb("e%d" % q, [128, 512], F32, ls) for q in range(4)] for _ in range(3)]
                stats = [self.ps[6], self.ps[7]]

                def epi_o(c, ps):
                    self.resid_epilogue(l, 0, tile, c, ps.t[:, :n], ps, self.cur, self.cur_name, bufs[c % 3], stats, c == 0, c == KC - 1)

                self.gemm_fm(w["w_o"], 16, KC, 256, oin, n, wb, self.ps[0:4], epi_o)
                self.ln_finish_stats(n, stats)
            S.barrier()
            self.ffn_tail(l, tile)

    def layer3(self, l):
        S = self.S
        na = self.na
        self.tiles_all = TILES
        self.qkv_project(l, na, rope=False)
        scale = 128 ** -0.5
        with contextlib.ExitStack() as ls:
            mask = S.sb("namask", [128, 15, 64], F32, ls)
            S.dma("sp", mask.t[:].rearrange("p a q -> p (a q)"), na["mask"], writes=[mask])
            kT = [S.sb("nkT", [128, NTOK], BF16, ls) for _ in range(2)]
            qT = [S.sb("nqT", [128, L], BF16, ls) for _ in range(2)]
            Ve = [S.sb("nVe", [128, 34, 128], BF16, ls) for _ in range(2)]
            Vo = [S.sb("nVo", [128, 33, 128], BF16, ls) for _ in range(2)]
            TT = [S.sb("nTT", [128, 15, 64], F32, ls) for _ in range(2)]
            tmp = [S.sb("ntmp", [128, 256], F32, ls) for _ in range(3)]
            Eb = [S.sb("nE", [128, 384], BF16, ls) for _ in range(3)]
            rz = [S.sb("nrz", [128, 512], F32, ls) for _ in range(2)]
            oh = [S.sb("noh", [128, 512], BF16, ls) for _ in range(2)]
            qkall = [self.dr("qkT", t[0]) for t in TILES]
            vall = [self.dr("Vd", t[0]) for t in TILES]
            rows_needed = sorted(set((t[0] - CTX) // 64 + i for t in self.tiles if t[2] == 0 for i in range(t[1] // 64)))
            it = 0
            g8 = 0

            def load(h):
                i = h % 2
                S.dma("sp", kT[i].t[:], self.qkT[32 + h], reads=qkall, writes=[kT[i]])
                S.dma("sp", qT[i].t[:], self.qkT[h, :, CTX:], reads=qkall, writes=[qT[i]])
                S.dma("sp", Ve[i].t[:], self.Vd[:, h * 128:(h + 1) * 128].rearrange("(c p) e -> p c e", p=128), reads=vall, writes=[Ve[i]])
                S.dma("sp", Vo[i].t[:], self.Vd[64:64 + 33 * 128, h * 128:(h + 1) * 128].rearrange("(c p) e -> p c e", p=128), reads=vall, writes=[Vo[i]])
                S.dma("sp", TT[i].t[:].rearrange("p a q -> p (a q)"), na["bias"][h], writes=[TT[i]])
                S.op("pool", lambda e: e.tensor_tensor(TT[i].t[:], TT[i].t[:], mask.t[:], ALU.add), [TT[i], mask], [TT[i]])

            def comp(h):
                nonlocal it, g8
                i = h % 2
                k_, q_, ve, vo_, tt = kT[i], qT[i], Ve[i], Vo[i], TT[i]
                for r0 in range(0, GRID, 8):
                    rows = [r for r in range(r0, r0 + 8) if r in rows_needed]
                    if not rows:
                        continue
                    ops_, zps = self.ps[2 + g8 % 2], self.ps[4 + g8 % 2]
                    for r in rows:
                        rs = min(max(r - 4, 0), GRID - 8)
                        a0 = rs - r + 7
                        sp_ = self.ps[it % 2]
                        tm, E = tmp[it % 3], Eb[it % 3]
                        it += 1
                        q_ap = q_.t[:, 64 * r:64 * r + 64]
                        for j in range(6):
                            kc0 = CTX + 64 * (rs + 2 * j) if j < 4 else 128 * (j - 4)
                            S.mm(sp_.t[:, j * 64:(j + 1) * 64], k_.t[:, kc0:kc0 + 128], q_ap, start=True, stop=True,
                                 reads=[k_, q_], writes=[sp_], inc=(j == 5))
                        S.dve(lambda e: e.scalar_tensor_tensor(tm.t[:].rearrange("p (j q) -> p j q", q=64),
                                                               sp_.t[:, 0:256].rearrange("p (j q) -> p j q", q=64), scale,
                                                               tt.t[:, a0:a0 + 7:2, :], ALU.mult, ALU.add), [sp_, tt], [tm])
                        S.act(E.t[:, 0:256], tm.t[:], AF.Exp, [tm], [E])
                        S.act(E.t[:, 256:384], sp_.t[:, 256:384], AF.Exp, [sp_], [E], scale=scale)
                        co = (r - r0) * 64
                        for j in range(6):
                            if j < 4:
                                kr = rs + 2 * j
                                v_ap = ve.t[:, 2 + kr // 2, :] if rs % 2 == 0 else vo_.t[:, (3 + kr) // 2, :]
                            else:
                                v_ap = ve.t[:, j - 4, :]
                            S.mm(ops_.t[:, co:co + 64], v_ap, E.t[:, j * 64:(j + 1) * 64], start=(j == 0), stop=(j == 5),
                                 reads=[ve, vo_, E], writes=[ops_], inc=False)
                        for j in range(6):
                            S.mm(zps.t[:, co:co + 64], self.ones_b.t[:], E.t[:, j * 64:(j + 1) * 64], start=(j == 0), stop=(j == 5),
                                 reads=[self.ones_b, E], writes=[zps], inc=(j == 5))
                    c_lo, c_hi = (rows[0] - r0) * 64, (rows[-1] - r0 + 1) * 64
                    z_, o_ = rz[g8 % 2], oh[g8 % 2]
                    g8 += 1
                    S.dve(lambda e: e.reciprocal(z_.t[:, c_lo:c_hi], zps.t[:, c_lo:c_hi]), [zps], [z_])
                    S.dve(lambda e: e.tensor_tensor(o_.t[:, c_lo:c_hi], ops_.t[:, c_lo:c_hi], z_.t[:, c_lo:c_hi], ALU.mult), [ops_, z_], [o_])
                    tcol = CTX + 64 * r0
                    S.dma("act", self.oT[h, :, tcol + c_lo:tcol + c_hi], o_.t[:, c_lo:c_hi], reads=[o_],
                          writes=[self.dr("oT", CTX + NT * (r0 // 8))])

            pipelined(32, load, comp, 1)
        S.barrier()
        self.attn_out_tail(l, na, [t for t in self.tiles if t[2] == 0])

    def layer1(self, l):
        S = self.S
        da = self.da
        self.tiles_all = TILES
        self.qkv_project(l, da, rope=True)
        scale = 128 ** -0.5
        lam_init = 0.8 - 0.6 * math.exp(-0.3 * l)
        with contextlib.ExitStack() as ls:
            lp = S.sb("lp", [128, 4], F32, ls)
            pr = S.sb("pr", [128, 2], F32, ls)
            ee = S.sb("ee", [128, 2], F32, ls)
            neglam = S.sb("neglam", [128, 1], F32, ls)
            gsc = S.sb("gsc", [128, 2], F32, ls)
            S.dma("sp", lp.t[:], da["lam"], writes=[lp])
            S.dma("sp", gsc.t[:], da["subln"], writes=[gsc])
            S.dve(lambda e: e.tensor_tensor(pr.t[:, 0:1], lp.t[:, 0:1], lp.t[:, 1:2], ALU.mult), [lp], [pr])
            S.dve(lambda e: e.tensor_tensor(pr.t[:, 1:2], lp.t[:, 2:3], lp.t[:, 3:4], ALU.mult), [lp], [pr])
            S.mm(self.ps[0].t[:, 0:2], self.ones_f.t[:], pr.t[:], start=True, stop=True, reads=[pr, self.ones_f], writes=[self.ps[0]])
            S.act(ee.t[:], self.ps[0].t[:, 0:2], AF.Exp, [self.ps[0]], [ee])
            S.dve(lambda e: e.scalar_tensor_tensor(neglam.t[:], ee.t[:, 1:2], -lam_init, ee.t[:, 0:1], ALU.add, ALU.subtract), [ee], [neglam])
            S.dve(lambda e: e.tensor_scalar_mul(gsc.t[:], gsc.t[:], 1.0 - lam_init), [gsc], [gsc])
            kT = [S.sb("dkT", [128, 2, NTOK], BF16, ls) for _ in range(2)]
            qT = [S.sb("dqT", [128, 2, NTOK], BF16, ls) for _ in range(2)]
            Vh = [S.sb("dVh", [128, 34, 256], BF16, ls) for _ in range(2)]
            Eb = [S.sb("dE", [128, 512], BF16, ls) for _ in range(3)]
            rz = S.sb("drz", [128, 512], F32, ls)
            On = [S.sb("dOn", [128, 2, 512], F32, ls) for _ in range(2)]
            comb = S.sb("dcomb", [128, 2, 512], F32, ls)
            sq = S.sb("dsq", [128, 2, 512], F32, ls)
            rr = S.sb("drr", [128, 512], F32, ls)
            ob = [S.sb("dob", [128, 2, 512], BF16, ls) for _ in range(2)]
            qkall = [self.dr("qkT", t[0]) for t in TILES]
            vall = [self.dr("Vd", t[0]) for t in TILES]
            it = 0
            nq = 0

            def load(h):
                i = h % 2
                S.dma("sp", kT[i].t[:], self.qkT[32 + 2 * h:34 + 2 * h].rearrange("c p t -> p c t"), reads=qkall, writes=[kT[i]])
                S.dma("sp", qT[i].t[:], self.qkT[2 * h:2 * h + 2].rearrange("c p t -> p c t"), reads=qkall, writes=[qT[i]])
                S.dma("sp", Vh[i].t[:], self.Vd[:, h * 256:(h + 1) * 256].rearrange("(c p) e -> p c e", p=128), reads=vall, writes=[Vh[i]])

            def comp(h):
                nonlocal it, nq
                i = h % 2
                k_, q_, v_ = kT[i], qT[i], Vh[i]
                for tile in self.tiles:
                    c0, n, r = tile
                    kcs = list(range(2)) if r == 1 else list(range(34))
                    for sub in range(2):
                        O0, O1, Z = self.ps[2 + 3 * sub], self.ps[3 + 3 * sub], self.ps[4 + 3 * sub]
                        for ki, kc in enumerate(kcs):
                            sp_ = self.ps[it % 2]
                            E = Eb[it % 3]
                            it += 1
                            S.mm(sp_.t[:, :n], k_.t[:, sub, kc * 128:(kc + 1) * 128], q_.t[:, sub, c0:c0 + n], start=True, stop=True,
                                 reads=[k_, q_], writes=[sp_])
                            S.act(E.t[:, :n], sp_.t[:, :n], AF.Exp, [sp_], [E], scale=scale)
                            first, last = ki == 0, ki == len(kcs) - 1
                            S.mm(O0.t[:, :n], v_.t[:, kc, 0:128], E.t[:, :n], start=first, stop=last, reads=[v_, E], writes=[O0], inc=False)
                            S.mm(O1.t[:, :n], v_.t[:, kc, 128:256], E.t[:, :n], start=first, stop=last, reads=[v_, E], writes=[O1], inc=False)
                            S.mm(Z.t[:, :n], self.ones_b.t[:], E.t[:, :n], start=first, stop=last, reads=[self.ones_b, E], writes=[Z], inc=True)
                        S.dve(lambda e: e.reciprocal(rz.t[:, :n], Z.t[:, :n]), [Z], [rz])
                        S.dve(lambda e: e.tensor_tensor(On[sub].t[:, 0, :n], O0.t[:, :n], rz.t[:, :n], ALU.mult), [O0, rz], [On[sub]])
                        S.dve(lambda e: e.tensor_tensor(On[sub].t[:, 1, :n], O1.t[:, :n], rz.t[:, :n], ALU.mult), [O1, rz], [On[sub]])
                    S.dve(lambda e: e.scalar_tensor_tensor(comb.t[:, :, :n], On[1].t[:, :, :n], neglam.t[:, 0:1], On[0].t[:, :, :n], ALU.mult, ALU.add),
                          [On[0], On[1], neglam], [comb])
                    S.act(sq.t[:, :, :n], comb.t[:, :, :n], AF.Square, [comb], [sq])
                    mp = self.ps[4]
                    for ec in range(2):
                        S.mm(mp.t[:, :n], self.ones_f.t[:], sq.t[:, ec, :n], start=(ec == 0), stop=(ec == 1), reads=[sq, self.ones_f], writes=[mp], inc=(ec == 1))
                    S.act(rr.t[:, :n], mp.t[:, :n], AF.Sqrt, [mp], [rr], scale=1.0 / 256.0, bias=self.eps_t.t[:, 0:1])
                    S.dve(lambda e: e.reciprocal(rr.t[:, :n], rr.t[:, :n]), [rr], [rr])
                    o_ = ob[nq % 2]
                    nq += 1
                    for ec in range(2):
                        S.dve(lambda e: e.scalar_tensor_tensor(o_.t[:, ec, :n], comb.t[:, ec, :n], gsc.t[:, ec:ec + 1], rr.t[:, :n], ALU.mult, ALU.mult),
                              [comb, gsc, rr], [o_])
                    S.dma("act", self.oT[2 * h:2 * h + 2, :, c0:c0 + n].rearrange("c p t -> p c t"), o_.t[:, :, :n], reads=[o_],
                          writes=[self.dr("oT", c0)])

            pipelined(16, load, comp, 1)
        S.barrier()
        self.attn_out_tail(l, da, self.tiles)

    def declare_s5_inputs(self):
        if 0 in self.layers:
            self.s5 = {
                "par": self.din("s5_par", [128, 2 * 3 * 128]),
                "BT": self.din("s5_BT", [128, 32, 2 * 128]),
                "CT": self.din("s5_CT", [128, 2 * 128 * 32]),
                "d": self.din("s5_d", [128, KC]),
                "w_glu": self.din("s5_wglu", [KC, 128, 2 * KC * 128]),
            }
            self.hbf = self.dtmp("hbf", [KC, 128, NTOK], BF16)
            self.yT = self.dtmp("yT", [KC, 128, NTOK])

    def layer0(self, l):
        S = self.S
        s5 = self.s5
        PI = math.pi
        for tile in TILES:
            c0, n, r = tile
            with contextlib.ExitStack() as ls:
                hT = S.sb("hT", [128, KC, 512], BF16, ls)
                self.mod_to_hT(l, 0, self.cur, self.cur_name, tile, hT, ls)
                S.dma("act", self.hbf[:, :, c0:c0 + n].rearrange("k p t -> p k t"), hT.t[:, :, :n], reads=[hT], writes=[self.dr("hbf", c0)])
            S.barrier()
        hall = [self.dr("hbf", t[0]) for t in TILES]
        with contextlib.ExitStack() as lsP:
            par = S.sb("par", [128, 2, 3, 128], F32, lsP)
            S.dma("sp", par.t[:].rearrange("p a b c -> p (a b c)"), s5["par"], writes=[par])
            rr_ = S.sb("s5r", [128, 2, 128], F32, lsP)
            CP = S.sb("s5CP", [128, 2, 11, 2, 128], F32, lsP)
            cf = S.sb("s5cf", [128, 2, 2, 128], F32, lsP)
            CT = S.sb("s5CT", [128, 2, 128, 32], BF16, lsP)
            negpi = S.sb("negpi", [128, 1], F32, lsP)
            S.dve(lambda e: e.memset(negpi.t[:], -PI), [], [negpi])
            S.dma("pool", CT.t[:].rearrange("p a b c -> p (a b c)"), s5["CT"], writes=[CT])
            S.op("pool", lambda e: e.tensor_scalar_mul(CT.t[:, 1], CT.t[:, 1], -1.0), [CT], [CT])
            with contextlib.ExitStack() as ls:
                def T_(nm):
                    return S.sb(nm, [128, 2, 128], F32, ls)
                dt, ar, th, a1, a2, cth, sth, abr, abi, den, n1, n2 = [T_("pp%d" % i) for i in range(12)]
                are, aim, ldt = par.t[:, :, 0, :], par.t[:, :, 1, :], par.t[:, :, 2, :]
                S.act(dt.t[:], ldt, AF.Exp, [par], [dt])
                S.dve(lambda e: e.tensor_tensor(ar.t[:], are, dt.t[:], ALU.mult), [par, dt], [ar])
                S.dve(lambda e: e.tensor_tensor(th.t[:], aim, dt.t[:], ALU.mult), [par, dt], [th])
                S.act(rr_.t[:], ar.t[:], AF.Exp, [ar], [rr_])
                for (a_, off) in ((a1, PI), (a2, 1.5 * PI)):
                    S.dve(lambda e: e.tensor_scalar_add(n2.t[:], th.t[:], off), [th], [n2])
                    S.dve(lambda e: e.tensor_copy(a_.t[:], n2.t[:]), [n2], [a_])
                    for m_ in range(1, 6):
                        S.dve(lambda e: e.tensor_scalar(n1.t[:], n2.t[:], 2 * PI * m_, 2 * PI, ALU.is_ge, ALU.mult), [n2], [n1])
                        S.dve(lambda e: e.tensor_tensor(a_.t[:], a_.t[:], n1.t[:], ALU.subtract), [a_, n1], [a_])
                S.act(sth.t[:], a1.t[:], AF.Sin, [a1, negpi], [sth], bias=negpi.t[:, 0:1])
                S.act(cth.t[:], a2.t[:], AF.Sin, [a2, negpi], [cth], bias=negpi.t[:, 0:1])
                S.dve(lambda e: e.tensor_copy(CP.t[:, :, 0, 0, :], cth.t[:]), [cth], [CP])
                S.dve(lambda e: e.tensor_copy(CP.t[:, :, 0, 1, :], sth.t[:]), [sth], [CP])
                for k in range(10):
                    cr, ci = CP.t[:, :, k, 0, :], CP.t[:, :, k, 1, :]
                    S.dve(lambda e: e.tensor_tensor(n1.t[:], cr, cr, ALU.mult), [CP], [n1])
                    S.dve(lambda e: e.tensor_tensor(n2.t[:], ci, ci, ALU.mult), [CP], [n2])
                    S.dve(lambda e: e.tensor_tensor(CP.t[:, :, k + 1, 0, :], n1.t[:], n2.t[:], ALU.subtract), [n1, n2], [CP])
                    S.dve(lambda e: e.scalar_tensor_tensor(CP.t[:, :, k + 1, 1, :], cr, 2.0, ci, ALU.mult, ALU.mult), [CP], [CP])
                S.dve(lambda e: e.tensor_tensor(abr.t[:], rr_.t[:], cth.t[:], ALU.mult), [rr_, cth], [abr])
                S.dve(lambda e: e.tensor_scalar_add(abr.t[:], abr.t[:], -1.0), [abr], [abr])
                S.dve(lambda e: e.tensor_tensor(abi.t[:], rr_.t[:], sth.t[:], ALU.mult), [rr_, sth], [abi])
                S.dve(lambda e: e.tensor_tensor(den.t[:], are, are, ALU.mult), [par], [den])
                S.dve(lambda e: e.tensor_tensor(n1.t[:], aim, aim, ALU.mult), [par], [n1])
                S.dve(lambda e: e.tensor_tensor(den.t[:], den.t[:], n1.t[:], ALU.add), [den, n1], [den])
                S.dve(lambda e: e.reciprocal(den.t[:], den.t[:]), [den], [den])
                S.dve(lambda e: e.tensor_tensor(n1.t[:], abr.t[:], are, ALU.mult), [abr, par], [n1])
                S.dve(lambda e: e.tensor_tensor(n2.t[:], abi.t[:], aim, ALU.mult), [abi, par], [n2])
                S.dve(lambda e: e.tensor_tensor(n1.t[:], n1.t[:], n2.t[:], ALU.add), [n1, n2], [n1])
                S.dve(lambda e: e.tensor_tensor(cf.t[:, :, 0, :], n1.t[:], den.t[:], ALU.mult), [n1, den], [cf])
                S.dve(lambda e: e.tensor_tensor(n1.t[:], abi.t[:], are, ALU.mult), [abi, par], [n1])
                S.dve(lambda e: e.tensor_tensor(n2.t[:], abr.t[:], aim, ALU.mult), [abr, par], [n2])
                S.dve(lambda e: e.tensor_tensor(n1.t[:], n1.t[:], n2.t[:], ALU.subtract), [n1, n2], [n1])
                S.dve(lambda e: e.tensor_tensor(cf.t[:, :, 1, :], n1.t[:], den.t[:], ALU.mult), [n1, den], [cf])
            S.barrier()
            nCre = S.sb("s5nC", [128, 128, 32], BF16, lsP)
            S.op("pool", lambda e: e.tensor_scalar_mul(nCre.t[:], CT.t[:, 0], -1.0), [CT], [nCre])
            segs_f = [(0, CTX)] + [(CTX + 1024 * i, CTX + 1024 * (i + 1)) for i in range(4)]
            segs = {0: segs_f, 1: [(0, CTX)] + segs_f[:0:-1]}
            with contextlib.ExitStack() as ls:
                uT = S.sb("s5u", [32, NTOK], BF16, ls)
                BT = [S.sb("s5B", [32, 2, 128], BF16, ls) for _ in range(2)]
                M = [[S.sb("s5M%d" % p, [128, 1025], F32, ls) for p in range(2)] for _ in range(2)]
                Wz = [[S.sb("s5W%d" % p, [128, 1024], F32, ls) for p in range(2)] for _ in range(2)]
                mt = [S.sb("s5mt", [128, 512], F32, ls) for _ in range(2)]
                Pt = [[S.sb("s5P%d" % p, [128, 512], F32, ls) for p in range(4)] for _ in range(2)]
                Z = [[S.sb("s5Z%d" % p, [128, 1024], F32, ls) for p in range(2)] for _ in range(2)]
                G = [S.sb("s5G%d" % p, [128, 1024], F32, ls) for p in range(2)]
                Q = [[S.sb("s5Q%d" % p, [128, 1024], BF16, ls) for p in range(4)] for _ in range(2)]
                carry = [S.sb("s5c", [128, 2], F32, ls) for _ in range(2)]
                ctmp = S.sb("s5ct", [128, 2], F32, ls)
                yacc = S.sb("s5y", [32, NTOK], F32, ls)
                it = 0
                sg = 0
                cb = 0
                yi = 0

                def load(st):
                    S.dma("pool", BT[st % 2].t[:].rearrange("c a p -> c (a p)"), s5["BT"][st], writes=[BT[st % 2]])

                def tables(st, d_):
                        Mre, Mim = M[d_]
                        Wre, Wim = Wz[d_]
                        def cp(k, part):
                            return CP.t[:, d_, k, part, st:st + 1]
                        S.op("pool", lambda e: e.memset(Mre.t[:, 0:1], 1.0), [], [Mre])
                        S.op("pool", lambda e: e.memset(Mim.t[:, 0:1], 0.0), [], [Mim])
                        ta, tb = mt[0], mt[1]
                        for k in range(10):
                            nn = 1 << k
                            S.act(ta.t[:, :nn], Mre.t[:, 0:nn], AF.Copy, [Mre, CP], [ta], scale=cp(k, 0))
                            S.act(tb.t[:, :nn], Mim.t[:, 0:nn], AF.Copy, [Mim, CP], [tb], scale=cp(k, 1))
                            S.op("pool", lambda e: e.tensor_tensor(Mre.t[:, nn:2 * nn], ta.t[:, :nn], tb.t[:, :nn], ALU.subtract), [ta, tb], [Mre])
                            S.act(ta.t[:, :nn], Mre.t[:, 0:nn], AF.Copy, [Mre, CP], [ta], scale=cp(k, 1))
                            S.act(tb.t[:, :nn], Mim.t[:, 0:nn], AF.Copy, [Mim, CP], [tb], scale=cp(k, 0))
                            S.op("pool", lambda e: e.tensor_tensor(Mim.t[:, nn:2 * nn], ta.t[:, :nn], tb.t[:, :nn], ALU.add), [ta, tb], [Mim])
                        S.act(Mre.t[:, 1024:1025], cp(10, 0), AF.Copy, [CP], [Mre])
                        S.act(Mim.t[:, 1024:1025], cp(10, 1), AF.Copy, [CP], [Mim])
                        cfr, cfi = cf.t[:, d_, 0, st:st + 1], cf.t[:, d_, 1, st:st + 1]
                        for h0 in range(0, 1024, 512):
                            S.act(ta.t[:], Mre.t[:, h0:h0 + 512], AF.Copy, [Mre, cf], [ta], scale=cfr)
                            S.act(tb.t[:], Mim.t[:, h0:h0 + 512], AF.Copy, [Mim, cf], [tb], scale=cfi)
                            S.op("pool", lambda e: e.tensor_tensor(Wre.t[:, h0:h0 + 512], ta.t[:], tb.t[:], ALU.add), [ta, tb], [Wre])
                            S.act(ta.t[:], Mre.t[:, h0:h0 + 512], AF.Copy, [Mre, cf], [ta], scale=cfi)
                            S.act(tb.t[:], Mim.t[:, h0:h0 + 512], AF.Copy, [Mim, cf], [tb], scale=cfr)
                            S.op("pool", lambda e: e.tensor_tensor(Wim.t[:, h0:h0 + 512], ta.t[:], tb.t[:], ALU.subtract), [ta, tb], [Wim])

                def comp(st):
                    nonlocal it, sg, cb, yi
                    u_, b_ = uT, BT[st % 2]
                    S.dma("sp", u_.t[:], self.hbf[st // 4, 32 * (st % 4):32 * (st % 4) + 32, :], reads=hall, writes=[u_])
                    for d_ in range(2):
                        Mre, Mim = M[d_]
                        Wre, Wim = Wz[d_]
                        if d_ == 0:
                            tables(st, 1)
                        elif st + 1 < 128:
                            tables(st + 1, 0)
                        prev_carry = None
                        for (c0, c1) in segs[d_]:
                            Ls = c1 - c0
                            Zre, Zim = Z[sg % 2]
                            Q1, Q2, Q3, Q4 = Q[sg % 2]
                            Gre, Gim = G
                            sg += 1
                            rev = d_ == 1
                            for k0 in range(c0, c1, 512):
                                k1 = min(k0 + 512, c1)
                                w_ = k1 - k0
                                pr_, pi_ = self.ps[(it % 2) * 2], self.ps[(it % 2) * 2 + 1]
                                p1, p2, p3, p4 = Pt[it % 2]
                                it += 1
                                S.mm(pr_.t[:, :w_], b_.t[:, 0, :], u_.t[:, k0:k1], start=True, stop=True, reads=[b_, u_], writes=[pr_])
                                S.mm(pi_.t[:, :w_], b_.t[:, 1, :], u_.t[:, k0:k1], start=True, stop=True, reads=[b_, u_], writes=[pi_])
                                if not rev:
                                    lo, hi = k0 - c0, k1 - c0
                                    wr, wi = Wre.t[:, lo:hi], Wim.t[:, lo:hi]
                                else:
                                    lo, hi = c1 - k1, c1 - k0
                                    wr, wi = Wre.t[:, lo:hi][:, ::-1], Wim.t[:, lo:hi][:, ::-1]
                                zo = slice(k0 - c0, k1 - c0)
                                S.dve(lambda e: e.tensor_tensor(p1.t[:, :w_], pr_.t[:, :w_], wr, ALU.mult), [pr_, Wre], [p1])
                                S.dve(lambda e: e.tensor_tensor(p2.t[:, :w_], pi_.t[:, :w_], wi, ALU.mult), [pi_, Wim], [p2])
                                S.dve(lambda e: e.tensor_tensor(p3.t[:, :w_], pi_.t[:, :w_], wr, ALU.mult), [pi_, Wre], [p3])
                                S.dve(lambda e: e.tensor_tensor(p4.t[:, :w_], pr_.t[:, :w_], wi, ALU.mult), [pr_, Wim], [p4])
                                S.op("pool", lambda e: e.tensor_tensor(Zre.t[:, zo], p1.t[:, :w_], p2.t[:, :w_], ALU.subtract), [p1, p2], [Zre])
                                S.op("pool", lambda e: e.tensor_tensor(Zim.t[:, zo], p3.t[:, :w_], p4.t[:, :w_], ALU.add), [p3, p4], [Zim])
                            rb = rr_.t[:, d_, st:st + 1].to_broadcast([128, Ls])
                            for (Zp, Gp, ci_) in ((Zre, Gre, 0), (Zim, Gim, 1)):
                                init = 0.0 if prev_carry is None else prev_carry.t[:, ci_:ci_ + 1]
                                rd = [Zp, rr_] + ([] if prev_carry is None else [prev_carry])
                                if not rev:
                                    S.dve(lambda e: e.tensor_tensor_scan(Gp.t[:, 0:Ls], rb, Zp.t[:, 0:Ls], init, ALU.mult, ALU.add), rd, [Gp])
                                else:
                                    S.dve(lambda e: e.tensor_tensor_scan(Gp.t[:, 0:Ls][:, ::-1], rb, Zp.t[:, 0:Ls][:, ::-1], init, ALU.mult, ALU.add), rd, [Gp])
                            e0 = 0 if rev else Ls - 1
                            cy = carry[cb % 2]
                            cb += 1
                            mr, mi = Mre.t[:, Ls:Ls + 1], Mim.t[:, Ls:Ls + 1]
                            S.dve(lambda e: e.tensor_scalar_mul(ctmp.t[:, 0:1], Gim.t[:, e0:e0 + 1], mi), [Gim, Mim], [ctmp])
                            S.dve(lambda e: e.scalar_tensor_tensor(cy.t[:, 0:1], Gre.t[:, e0:e0 + 1], mr, ctmp.t[:, 0:1], ALU.mult, ALU.subtract), [Gre, Mre, ctmp], [cy])
                            S.dve(lambda e: e.tensor_scalar_mul(ctmp.t[:, 1:2], Gre.t[:, e0:e0 + 1], mi), [Gre, Mim], [ctmp])
                            S.dve(lambda e: e.scalar_tensor_tensor(cy.t[:, 1:2], Gim.t[:, e0:e0 + 1], mr, ctmp.t[:, 1:2], ALU.mult, ALU.add), [Gim, Mre, ctmp], [cy])
                            prev_carry = cy
                            if not rev:
                                mre_, mim_ = Mre.t[:, 0:Ls], Mim.t[:, 0:Ls]
                            else:
                                mre_, mim_ = Mre.t[:, 0:Ls][:, ::-1], Mim.t[:, 0:Ls][:, ::-1]
                            S.dve(lambda e: e.tensor_tensor(Q1.t[:, :Ls], Gre.t[:, :Ls], mre_, ALU.mult), [Gre, Mre], [Q1])
                            S.dve(lambda e: e.tensor_tensor(Q2.t[:, :Ls], Gim.t[:, :Ls], mim_, ALU.mult), [Gim, Mim], [Q2])
                            S.dve(lambda e: e.tensor_tensor(Q3.t[:, :Ls], Gre.t[:, :Ls], mim_, ALU.mult), [Gre, Mim], [Q3])
                            S.dve(lambda e: e.tensor_tensor(Q4.t[:, :Ls], Gim.t[:, :Ls], mre_, ALU.mult), [Gim, Mre], [Q4])
                            for k0 in range(0, Ls, 512):
                                w_ = min(512, Ls - k0)
                                yp = self.ps[4 + yi % 2]
                                yi += 1
                                for qi, (Qx, lw) in enumerate(((Q1, CT.t[:, 0, st, :]), (Q2, nCre.t[:, st, :]), (Q3, CT.t[:, 1, st, :]), (Q4, CT.t[:, 1, st, :]))):
                                    S.mm(yp.t[0:32, :w_], lw, Qx.t[:, k0:k0 + w_], start=(qi == 0), stop=(qi == 3), reads=[CT, nCre, Qx], writes=[yp], inc=(qi == 3))
                                ys = yacc.t[:, c0 + k0:c0 + k0 + w_]
                                if d_ == 0:
                                    S.act(ys, yp.t[0:32, :w_], AF.Copy, [yp], [yacc])
                                else:
                                    S.dve(lambda e: e.tensor_tensor(ys, ys, yp.t[0:32, :w_], ALU.add), [yacc, yp], [yacc])
                    S.dma("act", self.yT[st // 4, 32 * (st % 4):32 * (st % 4) + 32, :], yacc.t[:], reads=[yacc], writes=[self.dr("yT", t[0]) for t in TILES])

                tables(0, 0)
                pipelined(128, load, comp, 1)
        S.barrier()
        with contextlib.ExitStack() as lsC:
            dsk = S.sb("s5d", [128, KC], F32, lsC)
            S.dma("sp", dsk.t[:], s5["d"], writes=[dsk])
            mv = self.modv[l]
            for tile in self.tiles:
                c0, n, r = tile
                with contextlib.ExitStack() as ls0:
                    gT = S.sb("gT", [128, KC, 512], BF16, ls0)
                    with contextlib.ExitStack() as ls:
                        xb = [S.sb("fx", [128, 512], F32, ls) for _ in range(3)]
                        yb = [S.sb("fy", [128, 512], F32, ls) for _ in range(3)]
                        hb = [S.sb("fh", [128, 512], F32, ls) for _ in range(2)]

                        def loadf(j):
                            S.dma("sp", xb[j % 3].t[:, :n], self.cur[j, :, c0:c0 + n], reads=[self.dr(self.cur_name, c0)], writes=[xb[j % 3]])
                            S.dma("sp", yb[j % 3].t[:, :n], self.yT[j, :, c0:c0 + n], reads=[self.dr("yT", c0)], writes=[yb[j % 3]])

                        def compf(j):
                            x_, y_, h_ = xb[j % 3], yb[j % 3], hb[j % 2]
                            S.act(h_.t[:, :n], x_.t[:, :n], AF.Identity, [x_, mv], [h_], scale=mv.t[:, 1, j, r:r + 1], bias=mv.t[:, 0, j, r:r + 1])
                            S.dve(lambda e: e.scalar_tensor_tensor(h_.t[:, :n], h_.t[:, :n], dsk.t[:, j:j + 1], y_.t[:, :n], ALU.mult, ALU.add), [h_, dsk, y_], [h_])
                            S.act(gT.t[:, j, :n], h_.t[:, :n], AF.Gelu, [h_], [gT])

                        pipelined(KC, loadf, compf, 2)
                    S.barrier()
                    with contextlib.ExitStack() as ls:
                        wb = [S.sb("wgl", [128, 2 * KC * 128], BF16, ls) for _ in range(2)]
                        sgb = [S.sb("sgb", [128, 512], F32, ls) for _ in range(2)]
                        yj = [S.sb("yj", [128, 512], F32, ls) for _ in range(2)]
                        bufs = [[S.sb("e%d" % q, [128, 512], F32, ls) for q in range(4)] for _ in range(3)]
                        stats = [self.ps[6], self.ps[7]]
                        cnt = [0]

                        fresh = {}

                        def loadg(b):
                            fresh[b] = self.wload(s5["w_glu"], b, wb[b % 2], 2 * KC * 128)

                        def compg(b):
                            wt = wb[b % 2]
                            if fresh[b]:
                                self.wstore(s5["w_glu"], b, wt, 2 * KC * 128)
                            pa = self.ps[(cnt[0] % 2) * 2]
                            pg = self.ps[(cnt[0] % 2) * 2 + 1]
                            s_, y_ = sgb[cnt[0] % 2], yj[cnt[0] % 2]
                            cnt[0] += 1
                            for half, ps in ((0, pa), (1, pg)):
                                for kc in range(KC):
                                    o = (half * KC + kc) * 128
                                    S.mm(ps.t[:, :n], wt.t[:, o:o + 128], gT.t[:, kc, :n], start=(kc == 0), stop=(kc == KC - 1),
                                         reads=[wt, gT], writes=[ps], inc=(kc == KC - 1))
                            S.act(s_.t[:, :n], pg.t[:, :n], AF.Sigmoid, [pg], [s_])
                            S.dve(lambda e: e.tensor_tensor(y_.t[:, :n], pa.t[:, :n], s_.t[:, :n], ALU.mult), [pa, s_], [y_])
                            self.resid_epilogue(l, 0, tile, b, y_.t[:, :n], y_, self.cur, self.cur_name, bufs[b % 3], stats, b == 0, b == KC - 1)

                        pipelined(KC, loadg, compg, 1)
                        self.ln_finish_stats(n, stats)
                S.barrier()
                self.ffn_tail(l, tile)

    def adaln(self):
        S = self.S
        with contextlib.ExitStack() as ls:
            cv = S.sb("cv", [128, KC, 2], F32, ls)
            scv = S.sb("scv", [128, KC, 2], F32, ls)
            S.dma("sp", cv.t[:], self.cvT, writes=[cv])
            S.act(scv.t[:], cv.t[:], AF.Silu, [cv], [scv])
            dn = S.sb("dn", [128, KC, 256], F32, ls)
            ups = [S.sb("up", [128, 2, D], F32, ls) for _ in range(2)]
            bT = S.sb("bT", [128, 192], F32, ls)
            tT = S.sb("tT", [128, 2, 2], F32, ls)
            pi = 0
            for l in self.layers:
                W = self.W[l]
                S.dma("sp", dn.t[:], W["down"], writes=[dn])
                S.dma("sp", bT.t[:], W["b"], writes=[bT])
                ps = self.ps[0]
                for rc in range(2):
                    for kc in range(KC):
                        S.mm(ps.t[:, rc * 2:rc * 2 + 2], dn.t[:, kc, rc * 128:(rc + 1) * 128], scv.t[:, kc, :],
                             start=(kc == 0), stop=(kc == KC - 1), reads=[dn, scv], writes=[ps], inc=(kc == KC - 1))
                S.dve(lambda e: e.tensor_copy(tT.t[:].rearrange("p a b -> p (a b)"), ps.t[:, 0:4]), [ps], [tT])
                mv = self.modv[l]
                for j in range(6):
                    up = ups[pi % 2]
                    S.dma("sp", up.t[:], W["up"][:, :, j * D:(j + 1) * D], writes=[up])
                    mp = self.ps[1 + pi % 2]
                    pi += 1
                    for n in range(KC):
                        for rc in range(2):
                            S.mm(mp.t[:, n * 2:n * 2 + 2], up.t[:, rc, n * 128:(n + 1) * 128], tT.t[:, rc, :],
                                 start=(rc == 0), stop=(rc == 1), reads=[up, tT], writes=[mp], inc=(rc == 1 and n == KC - 1))
                    for r in range(2):
                        S.dve(lambda e: e.tensor_tensor(mv.t[:, j, :, r], mp.t[:, 0:2 * KC].rearrange("p (n r) -> p n r", r=2)[:, :, r],
                                                        bT.t[:, j * KC:(j + 1) * KC], ALU.add), [mp, bT], [mv])
                    if j in (1, 4):
                        S.dve(lambda e: e.tensor_scalar_add(mv.t[:, j], mv.t[:, j], 1.0), [mv], [mv])
        S.barrier()

    def mod_to_hT(self, l, which, src, src_name, tile, hT, ls, extra=None):
        S = self.S
        c0, n, r = tile
        mv = self.modv[l]
        jsh, jsc = (0, 1) if which == 0 else (3, 4)
        bufs = [S.sb("xl", [128, 8, 512], F32, ls) for _ in range(2)]
        for g in range(4):
            b = bufs[g % 2]
            S.dma("sp", b.t[:, :, :n], src[g * 8:(g + 1) * 8, :, c0:c0 + n].rearrange("k p t -> p k t"),
                  reads=[self.dr(src_name, c0)], writes=[b])
            for k in range(8):
                kc = g * 8 + k
                S.act(hT.t[:, kc, :n], b.t[:, k, :n], AF.Identity, [b, mv], [hT],
                      scale=mv.t[:, jsc, kc, r:r + 1], bias=mv.t[:, jsh, kc, r:r + 1])

    def resid_epilogue(self, l, which, tile, j, y_ap, y_res, src, src_name, bufs, stats, first, last):
        S = self.S
        c0, n, r = tile
        mv = self.modv[l]
        jg = 2 if which == 0 else 5
        xc, tmp, rt, rsq = bufs
        self.flush_stats()
        S.dma("sp", xc.t[:, :n], src[j, :, c0:c0 + n], reads=[self.dr(src_name, c0)], writes=[xc])
        S.act(tmp.t[:, :n], y_ap, AF.Identity, [y_res, mv], [tmp], scale=mv.t[:, jg, j, r:r + 1])
        S.dve(lambda e: e.scalar_tensor_tensor(rt.t[:, :n], xc.t[:, :n], ALPHA, tmp.t[:, :n], ALU.mult, ALU.add), [xc, tmp], [rt])
        S.act(rsq.t[:, :n], rt.t[:, :n], AF.Square, [rt], [rsq])
        S.dma("act", self.rT[j, :, c0:c0 + n], rt.t[:, :n], reads=[rt], writes=[self.dr("rT", c0)])

        def stats_mm():
            S.mm(stats[0].t[:, :n], self.ones_f.t[:], rt.t[:, :n], start=first, stop=last, reads=[rt, self.ones_f], writes=[stats[0]], inc=False)
            S.mm(stats[1].t[:, :n], self.ones_f.t[:], rsq.t[:, :n], start=first, stop=last, reads=[rsq, self.ones_f], writes=[stats[1]], inc=True)

        self._pstats.append(stats_mm)

    def flush_stats(self):
        for fn in self._pstats:
            fn()
        self._pstats = []

    def ln_finish_stats(self, n, stats):
        S = self.S
        self.flush_stats()
        m, rs = self.mean_t, self.rstd_t
        S.dve(lambda e: e.tensor_scalar_mul(m.t[:, :n], stats[0].t[:, :n], 1.0 / D), [stats[0]], [m])
        S.dve(lambda e: e.tensor_tensor(rs.t[:, :n], m.t[:, :n], m.t[:, :n], ALU.mult), [m], [rs])
        S.dve(lambda e: e.scalar_tensor_tensor(rs.t[:, :n], stats[1].t[:, :n], 1.0 / D, rs.t[:, :n], ALU.mult, ALU.subtract), [stats[1], rs], [rs])
        S.act(rs.t[:, :n], rs.t[:, :n], AF.Sqrt, [rs], [rs], bias=self.eps_t.t[:, 0:1])
        S.dve(lambda e: e.reciprocal(rs.t[:, :n], rs.t[:, :n]), [rs], [rs])

    def ln_apply(self, l, which, tile, dst, dst_name, ls, hT=None, lnext=None):
        S = self.S
        c0, n, r = tile
        m, rs = self.mean_t, self.rstd_t
        gi = (l * 2 + which) * KC
        rb = [S.sb("rl", [128, 512], F32, ls) for _ in range(3)]
        t2 = [S.sb("t2", [128, 512], F32, ls) for _ in range(2)]
        xn = [S.sb("xn", [128, 512], F32, ls) for _ in range(2)]
        mv = self.modv[l]

        def load(j):
            S.dma("sp", rb[j % 3].t[:, :n], self.rT[j, :, c0:c0 + n], reads=[self.dr("rT", c0)], writes=[rb[j % 3]])

        def comp(j):
            a, b, c = rb[j % 3], t2[j % 2], xn[j % 2]
            S.dve(lambda e: e.tensor_tensor(b.t[:, :n], a.t[:, :n], m.t[:, :n], ALU.subtract), [a, m], [b])
            S.dve(lambda e: e.tensor_tensor(b.t[:, :n], b.t[:, :n], rs.t[:, :n], ALU.mult), [b, rs], [b])
            S.act(c.t[:, :n], b.t[:, :n], AF.Identity, [b, self.lng, self.lnb], [c],
                  scale=self.lng.t[:, gi + j:gi + j + 1], bias=self.lnb.t[:, gi + j:gi + j + 1])
            S.dma("act", dst[j, :, c0:c0 + n], c.t[:, :n], reads=[c], writes=[self.dr(dst_name, c0)])
            if hT is not None:
                S.act(hT.t[:, j, :n], c.t[:, :n], AF.Identity, [c, mv], [hT],
                      scale=mv.t[:, 4, j, r:r + 1], bias=mv.t[:, 3, j, r:r + 1])

        pipelined(KC, load, comp, 2)

    def wload(self, w_ap, b, buf, width):
        S = self.S
        key = w_ap.tensor.name
        if not hasattr(self, "wc"):
            self.wc, self.wc_done = {}, {}
        if key not in self.wc:
            self.wc[key] = self.dtmp("wc_" + key, [w_ap.shape[0], 128, width], BF16)
            self.wc_done[key] = set()
        if b in self.wc_done[key]:
            S.dma("pool", buf.t[:, :width], self.wc[key][b], reads=[self.dr("wc_" + key, b)], writes=[buf])
            return False
        S.dma("pool", buf.t[:, :width], w_ap[b], writes=[buf])
        return True

    def wstore(self, w_ap, b, buf, width):
        key = w_ap.tensor.name
        self.S.dma("act", self.wc[key][b], buf.t[:, :width], reads=[buf], writes=[self.dr("wc_" + key, b)])
        self.wc_done[key].add(b)

    def gemm_fm(self, w_ap, nblk, kcn, nw, hT, n, wbufs, psums, epi):
        S = self.S
        nb = len(wbufs)
        sub_n = nw // 128
        cnt = [0]

        fresh = {}

        def load(b):
            fresh[b] = self.wload(w_ap, b, wbufs[b % nb], kcn * nw)

        def comp(b):
            wt = wbufs[b % nb]
            if fresh[b]:
                self.wstore(w_ap, b, wt, kcn * nw)
            for sub in range(sub_n):
                ps = psums[cnt[0] % len(psums)]
                cnt[0] += 1
                for kc in range(kcn):
                    S.mm(ps.t[:, :n], wt.t[:, kc * nw + sub * 128:kc * nw + (sub + 1) * 128], hT.t[:, kc, :n],
                         start=(kc == 0), stop=(kc == kcn - 1), reads=[wt, hT], writes=[ps], inc=(kc == kcn - 1))
                epi(b * sub_n + sub, ps)

        pipelined(nblk, load, comp, nb - 1)

    def ffn_tail(self, l, tile):
        S = self.S
        c0, n, r = tile
        W = self.W[l]
        with contextlib.ExitStack() as ls0:
            hid = S.sb("hid", [128, FC, 512], BF16, ls0)
            with contextlib.ExitStack() as ls:
                hT = S.sb("hT", [128, KC, 512], BF16, ls)
                self.ln_apply(l, 0, tile, self.mid, self.mid_name, ls, hT=hT)
                wb = [S.sb("wfi", [128, 2 * KC * 128], BF16, ls) for _ in range(2)]
                sg = [S.sb("sg", [128, 512], F32, ls) for _ in range(2)]
                cnt = [0]

                fresh = {}

                def load(b):
                    fresh[b] = self.wload(W["w_in"], b, wb[b % 2], 2 * KC * 128)

                def comp(b):
                    wt = wb[b % 2]
                    if fresh[b]:
                        self.wstore(W["w_in"], b, wt, 2 * KC * 128)
                    pg = self.ps[(cnt[0] % 2) * 2]
                    pu = self.ps[(cnt[0] % 2) * 2 + 1]
                    s = sg[cnt[0] % 2]
                    cnt[0] += 1
                    for half, ps in ((0, pg), (1, pu)):
                        for kc in range(KC):
                            o = (half * KC + kc) * 128
                            S.mm(ps.t[:, :n], wt.t[:, o:o + 128], hT.t[:, kc, :n], start=(kc == 0), stop=(kc == KC - 1),
                                 reads=[wt, hT], writes=[ps], inc=(kc == KC - 1))
                    S.act(s.t[:, :n], pg.t[:, :n], AF.Silu, [pg], [s])
                    S.dve(lambda e: e.tensor_tensor(hid.t[:, b, :n], s.t[:, :n], pu.t[:, :n], ALU.mult), [s, pu], [hid])

                pipelined(FC, load, comp, 1)
            S.barrier()
            with contextlib.ExitStack() as ls:
                wb = [S.sb("wfo", [128, FC * 128], BF16, ls) for _ in range(2)]
                bufs = [[S.sb("e%d" % k, [128, 512], F32, ls) for k in range(4)] for _ in range(3)]
                stats = [self.ps[6], self.ps[7]]

                def epi(c, ps):
                    self.resid_epilogue(l, 1, tile, c, ps.t[:, :n], ps, self.mid, self.mid_name, bufs[c % 3], stats, c == 0, c == KC - 1)

                self.gemm_fm(W["w_out"], KC, FC, 128, hid, n, wb, self.ps[0:4], epi)
                self.ln_finish_stats(n, stats)
        S.barrier()
        with contextlib.ExitStack() as ls:
            self.ln_apply(l, 1, tile, self.nxt, self.nxt_name, ls)
        S.barrier()

    def layer2(self, l):
        S = self.S
        cm = self.cm
        with contextlib.ExitStack() as lsC:
            lngb = S.sb("cmlngb", [128, 2 * KC], F32, lsC)
            wsT = S.sb("cmwsT", [128, 16 * 128], BF16, lsC)
            Cj = S.sb("cmCj", [128, KC, 128], F32, lsC)
            S.dma("sp", lngb.t[:], cm["lngb"], writes=[lngb])
            S.dma("pool", wsT.t[:], cm["wsT"], writes=[wsT])
            with contextlib.ExitStack() as ls:
                bsb = S.sb("bsb", [128, 16 * 128], F32, ls)
                rsum = S.sb("rsum", [128, 16 * 128], F32, ls)
                S.dma("sp", bsb.t[:], cm["bs"].partition_broadcast(128), writes=[bsb])
                for q in range(4):
                    ps = self.ps[q]
                    S.mm(ps.t[:, :], self.ones_b.t[:], wsT.t[:, q * 512:(q + 1) * 512], start=True, stop=True,
                         reads=[wsT, self.ones_b], writes=[ps])
                    S.dve(lambda e: e.tensor_copy(rsum.t[:, q * 512:(q + 1) * 512], ps.t[:, :]), [ps], [rsum])
                for j in range(KC):
                    g = j // 2
                    S.dve(lambda e: e.scalar_tensor_tensor(Cj.t[:, j, :], rsum.t[:, g * 128:(g + 1) * 128], lngb.t[:, KC + j:KC + j + 1],
                                                           bsb.t[:, g * 128:(g + 1) * 128], ALU.mult, ALU.add), [rsum, lngb, bsb], [Cj])
            S.barrier()
            for tile in self.tiles:
                c0, n, r = tile
                nch = n // 128
                with contextlib.ExitStack() as ls0:
                    hT = S.sb("hT", [128, KC, 512], BF16, ls0)
                    vn = S.sb("vn", [128, 4, D], BF16, ls0)
                    with contextlib.ExitStack() as ls:
                        self.mod_to_hT(l, 0, self.cur, self.cur_name, tile, hT, ls)
                    S.barrier()
                    with contextlib.ExitStack() as ls:
                        v = S.sb("v", [128, 4, D], F32, ls)
                        wb = [S.sb("wv", [128, KC * 256], BF16, ls) for _ in range(2)]
                        cnt = [0]

                        fresh = {}

                        def load(b):
                            fresh[b] = self.wload(cm["w_v"], b, wb[b % 2], KC * 256)

                        def comp(b):
                            wt = wb[b % 2]
                            if fresh[b]:
                                self.wstore(cm["w_v"], b, wt, KC * 256)
                            for ch in range(nch):
                                ps = self.ps[cnt[0] % 8]
                                cnt[0] += 1
                                for kc in range(KC):
                                    S.mm(ps.t[:, :256], hT.t[:, kc, ch * 128:(ch + 1) * 128], wt.t[:, kc * 256:(kc + 1) * 256],
                                         start=(kc == 0), stop=(kc == KC - 1), reads=[wt, hT], writes=[ps], inc=(kc == KC - 1))
                                S.act(v.t[:, ch, b * 256:(b + 1) * 256], ps.t[:, :256], AF.Gelu, [ps], [v])

                        pipelined(D // 256, load, comp, 1)
                        st6 = S.sb("st6", [128, 8, 6], F32, ls)
                        mvv = S.sb("mvv", [128, 4, 2], F32, ls)
                        nb_ = S.sb("nb_", [128, 4, 2], F32, ls)
                        for ch in range(nch):
                            for q in range(8):
                                S.dve(lambda e: e.bn_stats(st6.t[:, q, :], v.t[:, ch, q * 512:(q + 1) * 512]), [v], [st6])
                            S.dve(lambda e: e.bn_aggr(mvv.t[:, ch, :], st6.t[:].rearrange("p a b -> p (a b)")), [st6], [mvv])
                            S.act(nb_.t[:, ch, 0:1], mvv.t[:, ch, 1:2], AF.Sqrt, [mvv], [nb_], bias=self.eps_t.t[:, 0:1])
                            S.dve(lambda e: e.reciprocal(nb_.t[:, ch, 0:1], nb_.t[:, ch, 0:1]), [nb_], [nb_])
                            S.dve(lambda e: e.scalar_tensor_tensor(nb_.t[:, ch, 1:2], mvv.t[:, ch, 0:1], -1.0, nb_.t[:, ch, 0:1], ALU.mult, ALU.mult), [mvv, nb_], [nb_])
                            S.act(vn.t[:, ch, :], v.t[:, ch, :], AF.Identity, [v, nb_], [vn], scale=nb_.t[:, ch, 0:1], bias=nb_.t[:, ch, 1:2])
                    S.barrier()
                    with contextlib.ExitStack() as ls:
                        vmT = S.sb("vmT", [128, KC, 512], BF16, ls)
                        k = 0
                        for ch in range(nch):
                            for j in range(KC):
                                g = j // 2
                                ps = self.ps[k % 8]
                                k += 1
                                S.mm(ps.t[:, :128], vn.t[:, ch, j * 128:(j + 1) * 128], wsT.t[:, g * 128:(g + 1) * 128], start=True, stop=True,
                                     reads=[vn, wsT], writes=[ps])
                                S.dve(lambda e: e.scalar_tensor_tensor(vmT.t[:, j, ch * 128:(ch + 1) * 128], ps.t[:, :128], lngb.t[:, j:j + 1],
                                                                       Cj.t[:, j, :], ALU.mult, ALU.add), [ps, lngb, Cj], [vmT])
                        wb = [S.sb("wu", [128, KC * 256], BF16, ls) for _ in range(3)]
                        ug = [S.sb("ug", [128, 512], F32, ls) for _ in range(2)]

                        def epi_u(c, ps):
                            u_ = ug[c % 2]
                            S.act(u_.t[:, :n], ps.t[:, :n], AF.Gelu, [ps], [u_])
                            S.dve(lambda e: e.tensor_tensor(vmT.t[:, c, :n], u_.t[:, :n], vmT.t[:, c, :n], ALU.mult), [u_, vmT], [vmT])

                        self.gemm_fm(cm["w_u"], KC // 2, KC, 256, hT, n, wb, self.ps[0:4], epi_u)
                        bufs = [[S.sb("e%d" % q, [128, 512], F32, ls) for q in range(4)] for _ in range(3)]
                        stats = [self.ps[6], self.ps[7]]

                        def epi_o(c, ps):
                            self.resid_epilogue(l, 0, tile, c, ps.t[:, :n], ps, self.cur, self.cur_name, bufs[c % 3], stats, c == 0, c == KC - 1)

                        self.gemm_fm(cm["w_out"], KC // 2, KC, 256, vmT, n, wb, self.ps[0:4], epi_o)
                        self.ln_finish_stats(n, stats)
                S.barrier()
                self.ffn_tail(l, tile)


def tile_fm(W, nw):
    K, N = W.shape
    return np.ascontiguousarray(W.reshape(K // 128, 128, N // nw, nw).transpose(2, 1, 0, 3)).reshape(N // nw, 128, (K // 128) * nw)


def tile_pair(Wa, Wb):
    K, N = Wa.shape
    a = Wa.reshape(K // 128, 128, N // 128, 128).transpose(2, 1, 0, 3)
    b = Wb.reshape(K // 128, 128, N // 128, 128).transpose(2, 1, 0, 3)
    return np.ascontiguousarray(np.stack([a, b], axis=2)).reshape(N // 128, 128, 2 * (K // 128) * 128)


def fm_vec(v):
    sh = v.shape[:-1]
    return np.ascontiguousarray(np.moveaxis(v.reshape(sh + (KC, 128)), -1, 0))


def common_inputs(inp, layers):
    d = {}
    d["lngT"] = fm_vec(inp["ln_g"]).reshape(128, -1)
    d["lnbT"] = fm_vec(inp["ln_b"]).reshape(128, -1)
    for l in layers:
        d["ada_down%d" % l] = np.ascontiguousarray(inp["ada_down"][l].reshape(KC, 128, 256).transpose(1, 0, 2))
        d["ada_up%d" % l] = np.ascontiguousarray(inp["ada_up"][l].reshape(2, 128, 6 * D).transpose(1, 0, 2))
        d["ada_b%d" % l] = np.ascontiguousarray(inp["ada_b"][l].reshape(192, 128).T)
        wi = inp["ffn_w_in"][l]
        d["ffn_in%d" % l] = tile_pair(wi[:, :F], wi[:, F:])
        d["ffn_out%d" % l] = tile_fm(inp["ffn_w_out"][l], 128)
    if 2 in layers:
        w = inp["cm_w_in"][0]
        d["cm_wu"] = tile_fm(w[:, :D], 256)
        d["cm_wv"] = tile_fm(w[:, D:], 256)
        d["cm_wout"] = tile_fm(inp["cm_w_out"][0], 256)
        d["cm_lngb"] = np.concatenate([fm_vec(inp["cm_ln_g"][0]), fm_vec(inp["cm_ln_b"][0])], axis=1)
        d["cm_wsT"] = np.ascontiguousarray(inp["cm_w_s"][0].transpose(2, 0, 1)).reshape(128, 16 * 128)
        d["cm_bs"] = np.ascontiguousarray(inp["cm_b_s"][0].reshape(1, 16 * 128))
    if 0 in layers:
        G2 = 128
        def per_state(a):
            return a.reshape(2, G2, 128).transpose(2, 0, 1)
        are, aim = inp["ssm_a_re"][0], inp["ssm_a_im"][0]
        ldt = np.broadcast_to(inp["ssm_log_dt"][0][:, :, None], (2, 256, 64))
        par = np.stack([per_state(are), per_state(aim), per_state(np.ascontiguousarray(ldt))], axis=2)
        d["s5_par"] = np.ascontiguousarray(par).reshape(128, 2 * 3 * 128).astype(np.float32)
        BT = np.zeros((128, 32, 2, 128), np.float32)
        for part, nm in enumerate(("ssm_b_re", "ssm_b_im")):
            Bp = inp[nm][0].reshape(128, 2, 64, 16)
            for g2 in range(2):
                BT[:, g2 * 16:(g2 + 1) * 16, part, g2 * 64:(g2 + 1) * 64] = Bp[:, g2].transpose(0, 2, 1)
        d["s5_BT"] = BT.reshape(128, 32, 2 * 128)
        CT = np.zeros((128, 2, 128, 32), np.float32)
        for part, nm in enumerate(("ssm_c_re", "ssm_c_im")):
            Cp = inp[nm][0].reshape(128, 2, 16, 64)
            for g2 in range(2):
                CT[g2 * 64:(g2 + 1) * 64, part, :, g2 * 16:(g2 + 1) * 16] = Cp[:, g2].transpose(2, 0, 1)
        d["s5_CT"] = CT.reshape(128, 2 * 128 * 32)
        d["s5_d"] = fm_vec(inp["ssm_d"][0])
        w = inp["ssm_w_glu"][0]
        d["s5_wglu"] = tile_pair(w[:, :D], w[:, D:])
    if 1 in layers:
        w = inp["da_w_qkv"][0]
        d["da_wqk"] = tile_fm(w[:, :2 * D], 256)
        d["da_wv"] = tile_fm(w[:, 2 * D:], 256)
        d["da_wo"] = tile_fm(inp["da_w_o"][0], 256)
        d["da_lam"] = np.ascontiguousarray(inp["da_lambda"][0].T)
        d["da_subln"] = np.ascontiguousarray(inp["da_subln_g"][0].reshape(2, 128).T)
        t = np.arange(L)
        row, col = (t // GRID).astype(np.float32), (t % GRID).astype(np.float32)
        inv = (np.float32(10000.0) ** (-np.arange(32, dtype=np.float32) * np.float32(2.0) / np.float32(64))).astype(np.float32)
        C = np.zeros((128, L), np.float32)
        Sn = np.zeros((128, L), np.float32)
        P = np.zeros((128, 128), np.float32)
        for dd in range(128):
            pos = row if dd < 64 else col
            f = dd % 32
            ang = (pos * inv[f]).astype(np.float32)
            C[dd] = np.cos(ang)
            first = (dd % 64) < 32
            Sn[dd] = -np.sin(ang) if first else np.sin(ang)
            P[dd + 32 if first else dd - 32, dd] = 1.0
        d["ropeC"], d["ropeS"], d["ropeP"] = C, Sn, P
    if 3 in layers:
        w = inp["na_w_qkv"][0]
        d["na_wqk"] = tile_fm(w[:, :2 * D], 256)
        d["na_wv"] = tile_fm(w[:, 2 * D:], 256)
        d["na_wo"] = tile_fm(inp["na_w_o"][0], 256)
        rpb = inp["na_rpb"][0]
        kc = np.arange(64)[:, None]
        qc = np.arange(64)[None, :]
        cs = np.clip(qc - 8, 0, GRID - 16)
        valid = (kc >= cs) & (kc <= cs + 15)
        dc = np.clip(kc - qc + 15, 0, 30)
        T = rpb[:, :, dc] * valid[None, None]
        T = np.concatenate([T, np.zeros((32, 1, 64, 64), np.float32)], axis=1)
        TT = np.stack([T[:, 0:15], T[:, 1:16]], axis=1)
        d["na_bias"] = np.ascontiguousarray(TT.transpose(0, 1, 3, 2, 4)).reshape(32, 128, 15 * 64).astype(np.float32)
        m = np.where(valid, 0.0, -30000.0).astype(np.float32)
        M = np.broadcast_to(m[None, :, None, :], (2, 64, 15, 64)).copy()
        M[1, :, 14, :] = -30000.0
        d["na_mask"] = np.ascontiguousarray(M).reshape(128, 15 * 64)
    return d


def core_inputs(inp, b, x_lat=None, x_ctx=None):
    xl = inp["x"][b] if x_lat is None else x_lat
    xc = inp["ctx"][b] if x_ctx is None else x_ctx
    xt = np.concatenate([xc, xl], axis=0)
    d = {"xT": np.ascontiguousarray(xt.T).reshape(KC, 128, NTOK)}
    cv = np.stack([inp["c"][b], inp["c_ctx"]], axis=0)
    d["cvT"] = np.ascontiguousarray(fm_vec(cv).transpose(0, 2, 1))
    return d


def kernel(**inputs):
    inp = {k: np.asarray(v) for k, v in inputs.items()}
    layers = [0, 1, 2, 3]
    prog = Prog(layers)
    nc = prog.build()
    com = common_inputs(inp, layers)
    in_maps = []
    for b in range(2):
        m = dict(com)
        m.update(core_inputs(inp, b))
        in_maps.append({k: m[k] for k in prog.in_names})
    res = run_bass_kernel_spmd(nc, in_maps, core_ids=[0, 1])
    out = np.empty((2, L, D), np.float32)
    for b in range(2):
        out[b] = res.results[b]["outT"].reshape(D, L).T
    return out
```
